# Optimizing a Trainium2 kernel written in Bass

```python
import math
import jax, jax.numpy as jnp
from jax import lax
import numpy as np

D_MODEL = 1024
BATCH = 8
SEQ = 2048
DEPTH = 1

CHUNK = 64
Q_BLOCK = 128
HEAD_DIM = 64
V_DIM = 2 * HEAD_DIM
N_HEADS = D_MODEL // V_DIM
QK_WIDTH = N_HEADS * 2 * HEAD_DIM
ATTN_WIDTH = N_HEADS * V_DIM
CONV_WIDTH = D_MODEL
CONV_K = 3
ROPE_THETA = 10000.0
LN_EPS = 1e-5
RMS_EPS = 1e-5
DN_ALPHA = (2.0 * DEPTH) ** 0.25
DN_BETA = (8.0 * DEPTH) ** -0.25

SPLIT_WIDTHS = (QK_WIDTH, QK_WIDTH, ATTN_WIDTH, ATTN_WIDTH,
                CONV_WIDTH, CONV_WIDTH, CONV_WIDTH, CONV_WIDTH, 2 * D_MODEL)
IN_WIDTH = sum(SPLIT_WIDTHS)
SPLIT_POINTS = tuple(int(v) for v in np.cumsum(SPLIT_WIDTHS)[:-1])

kernel_name = "hybrid_diffattn_shortconv_gated_deepnorm"


def layer_norm(x, g, b):
    xf = x.astype(jnp.float32)
    mu = jnp.mean(xf, axis=-1, keepdims=True)
    var = jnp.mean(jnp.square(xf - mu), axis=-1, keepdims=True)
    y = (xf - mu) * lax.rsqrt(var + LN_EPS)
    return (y * g.astype(jnp.float32) + b.astype(jnp.float32)).astype(x.dtype)


def rms_norm(x, g):
    xf = x.astype(jnp.float32)
    y = xf * lax.rsqrt(jnp.mean(jnp.square(xf), axis=-1, keepdims=True) + RMS_EPS)
    return (y * g.astype(jnp.float32)).astype(x.dtype)


def rotary(t, seq_len):
    half = HEAD_DIM // 2
    inv_freq = 1.0 / (ROPE_THETA ** (jnp.arange(half, dtype=jnp.float32) / half))
    pos = jnp.arange(seq_len, dtype=jnp.float32)
    ang = pos[:, None] * inv_freq[None, :]
    cos = jnp.concatenate([jnp.cos(ang), jnp.cos(ang)], -1)[None, :, None, None, :]
    sin = jnp.concatenate([jnp.sin(ang), jnp.sin(ang)], -1)[None, :, None, None, :]
    t1, t2 = t[..., :half], t[..., half:]
    rot = jnp.concatenate([-t2, t1], axis=-1)
    return (t * cos.astype(t.dtype) + rot * sin.astype(t.dtype))


def diff_attention(q, k, v, lam):
    seq_len = q.shape[1]
    scale = HEAD_DIM ** -0.5
    outs = []
    for i in range(seq_len // Q_BLOCK):
        start, end = i * Q_BLOCK, (i + 1) * Q_BLOCK
        qb = q[:, start:end].astype(jnp.float32) * scale
        kb = k[:, :end].astype(jnp.float32)
        vb = v[:, :end].astype(jnp.float32)
        s = jnp.einsum('bqhcd,bkhcd->bhcqk', qb, kb)
        qpos = start + jnp.arange(Q_BLOCK)
        kpos = jnp.arange(end)
        allowed = (kpos[None, :] // CHUNK) <= (qpos[:, None] // CHUNK)
        s = jnp.where(allowed[None, None, None], s, -jnp.inf)
        p = jax.nn.softmax(s, axis=-1)
        a = p[:, :, 0] - lam * p[:, :, 1]
        outs.append(jnp.einsum('bhqk,bkhe->bqhe', a, vb).astype(v.dtype))
    return jnp.concatenate(outs, axis=1)


def causal_dwconv(u, w, b):
    rhs = w[:, None, :].astype(u.dtype)
    y = lax.conv_general_dilated(u, rhs, window_strides=(1,),
                                 padding=[(CONV_K - 1, 0)],
                                 dimension_numbers=('NWC', 'WIO', 'NWC'),
                                 feature_group_count=u.shape[-1])
    return y + b.astype(u.dtype)


def setup_inputs(seed: int = 0) -> dict:
    key = jax.random.key(seed)
    ks = jax.random.split(key, 16)
    f32 = jnp.float32
    x = jax.random.normal(ks[0], (BATCH, SEQ, D_MODEL), f32)
    w_in = jax.random.normal(ks[1], (DEPTH, D_MODEL, IN_WIDTH), f32) * D_MODEL ** -0.5
    v_lo = 2 * QK_WIDTH
    col_scale = jnp.ones((IN_WIDTH,), f32).at[v_lo:v_lo + ATTN_WIDTH].set(DN_BETA)
    w_in = w_in * col_scale
    b_gate = 0.01 * jax.random.normal(ks[2], (DEPTH, 2 * D_MODEL), f32)
    lambda_q1 = 0.1 * jax.random.normal(ks[3], (DEPTH, HEAD_DIM), f32)
    lambda_k1 = 0.1 * jax.random.normal(ks[4], (DEPTH, HEAD_DIM), f32)
    lambda_q2 = 0.1 * jax.random.normal(ks[5], (DEPTH, HEAD_DIM), f32)
    lambda_k2 = 0.1 * jax.random.normal(ks[6], (DEPTH, HEAD_DIM), f32)
    subln_g = 1.0 + 0.02 * jax.random.normal(ks[7], (DEPTH, V_DIM), f32)
    conv_w = jax.random.normal(ks[8], (DEPTH, CONV_K, CONV_WIDTH), f32) * CONV_K ** -0.5
    conv_b = 0.01 * jax.random.normal(ks[9], (DEPTH, CONV_WIDTH), f32)
    w_a_out = jax.random.normal(ks[10], (DEPTH, ATTN_WIDTH, D_MODEL), f32) * ATTN_WIDTH ** -0.5 * DN_BETA
    w_b_out = jax.random.normal(ks[11], (DEPTH, CONV_WIDTH, D_MODEL), f32) * CONV_WIDTH ** -0.5 * DN_BETA
    w_o = jax.random.normal(ks[12], (DEPTH, D_MODEL, D_MODEL), f32) * D_MODEL ** -0.5 * DN_BETA
    ln_g = 1.0 + 0.02 * jax.random.normal(ks[13], (DEPTH, D_MODEL), f32)
    ln_b = 0.01 * jax.random.normal(ks[14], (DEPTH, D_MODEL), f32)
    return {"x": x, "w_in": w_in, "b_gate": b_gate,
            "lambda_q1": lambda_q1, "lambda_k1": lambda_k1,
            "lambda_q2": lambda_q2, "lambda_k2": lambda_k2,
            "subln_g": subln_g, "conv_w": conv_w, "conv_b": conv_b,
            "w_a_out": w_a_out, "w_b_out": w_b_out, "w_o": w_o,
            "ln_g": ln_g, "ln_b": ln_b}


def reference(x, w_in, b_gate, lambda_q1, lambda_k1, lambda_q2, lambda_k2,
              subln_g, conv_w, conv_b, w_a_out, w_b_out, w_o, ln_g, ln_b):
    bsz, seq_len, _ = x.shape
    for l in range(DEPTH):
        lam_init = 0.8 - 0.6 * math.exp(-0.3 * l)
        proj = jnp.einsum('bsd,de->bse', x, w_in[l])
        q, k, v, z_a, h, bg, cg, z_b, gl = jnp.split(proj, SPLIT_POINTS, axis=-1)

        q = rotary(q.reshape(bsz, seq_len, N_HEADS, 2, HEAD_DIM), seq_len)
        k = rotary(k.reshape(bsz, seq_len, N_HEADS, 2, HEAD_DIM), seq_len)
        v = v.reshape(bsz, seq_len, N_HEADS, V_DIM)
        lam = (jnp.exp(jnp.sum(lambda_q1[l].astype(jnp.float32) * lambda_k1[l].astype(jnp.float32)))
               - jnp.exp(jnp.sum(lambda_q2[l].astype(jnp.float32) * lambda_k2[l].astype(jnp.float32)))
               + lam_init)
        o = diff_attention(q, k, v, lam)
        o = rms_norm(o, subln_g[l]) * (1.0 - lam_init)
        o = o.reshape(bsz, seq_len, ATTN_WIDTH) * jax.nn.silu(z_a)
        y_a = jnp.einsum('bse,ed->bsd', o, w_a_out[l])

        c = causal_dwconv(cg * h, conv_w[l], conv_b[l])
        c = bg * c * jax.nn.silu(z_b)
        y_b = jnp.einsum('bse,ed->bsd', c, w_b_out[l])

        g = jax.nn.sigmoid(gl + b_gate[l])
        g_a, g_b = g[..., :D_MODEL], g[..., D_MODEL:]
        merged = g_a * y_a + g_b * y_b
        out = jnp.einsum('bsd,de->bse', merged, w_o[l])

        x = layer_norm(DN_ALPHA * x + out, ln_g[l], ln_b[l])
    return x
```

```python
import math
from contextlib import ExitStack

import numpy as np
import concourse.bass as bass
import concourse.mybir as mybir
from concourse.bass_utils import run_bass_kernel_spmd

F32 = mybir.dt.float32
BF16 = mybir.dt.bfloat16
AF = mybir.ActivationFunctionType
ALU = mybir.AluOpType
AX = mybir.AxisListType

S = 2048
D = 1024
NT = 16
LN_EPS = 1e-5
RMS_EPS = 1e-5
DN_ALPHA = 2.0 ** 0.25
LAM_INIT = 0.8 - 0.6 * math.exp(-0.3 * 0)
ONE_M_LAM = 1.0 - LAM_INIT

C_COS, C_SIN = 0, 512
C_BG, C_CW, C_CB, C_LAM, C_ID, C_GC = 1024, 1040, 1064, 1072, 1328, 1456
NCST = 1460


class Prog:
    def __init__(self, nc, stack):
        self.nc = nc
        self.stack = stack
        self.eng = {}
        self.sems = {}
        for n in ("pe", "act", "dve", "pool", "sp"):
            h = stack.enter_context(nc.semaphore("s_" + n))
            self.eng[n] = dict(sem=h, semname="s_" + n, count=0, seen={}, stream=[])
            self.sems["s_" + n] = h
        self.dcount = {}
        self.res = {}

    def dsem(self, name):
        if name not in self.sems:
            self.sems[name] = self.stack.enter_context(self.nc.semaphore(name))
            self.dcount[name] = 0
        return self.sems[name]

    def _collect(self, en, reads, writes, strict=False, own_sem=None):
        E = self.eng[en]
        need = {}

        def req(tok, raw):
            if tok is None:
                return
            sn, c, snap = tok
            if sn == own_sem:
                return
            if sn == E["semname"] and not strict:
                if (not raw) or en == "pe":
                    return
            if E["seen"].get(sn, 0) >= c:
                return
            if need.get(sn, (0, None))[0] < c:
                need[sn] = (c, snap)

        for k in reads:
            r = self.res.get(k)
            if r:
                req(r["w"], True)
        for k in writes:
            r = self.res.get(k)
            if r:
                req(r["w"], False)
                for tok in r["r"].values():
                    req(tok, False)
        return need

    def _apply_waits(self, en, need):
        E = self.eng[en]
        for sn, (c, snap) in need.items():
            h = self.sems[sn]
            E["stream"].append(lambda e, h=h, c=c: e.wait_ge(h, c))
            if E["seen"].get(sn, 0) < c:
                E["seen"][sn] = c
            for k2, v2 in snap.items():
                if E["seen"].get(k2, 0) < v2:
                    E["seen"][k2] = v2

    def _record(self, tok, reads, writes):
        sn = tok[0]
        for k in reads:
            r = self.res.setdefault(k, dict(w=None, r={}))
            r["r"][sn] = tok
        for k in writes:
            self.res[k] = dict(w=tok, r={})

    def op(self, en, fn, reads=(), writes=(), strict=False):
        self.group(en, [fn], reads, writes, strict)

    def group(self, en, fns, reads=(), writes=(), strict=False):
        E = self.eng[en]
        self._apply_waits(en, self._collect(en, reads, writes, strict=strict))
        for fn in fns[:-1]:
            E["stream"].append(lambda e, fn=fn: fn(e))
        E["count"] += 1
        c = E["count"]
        h = E["sem"]
        fn = fns[-1]
        E["stream"].append(lambda e, fn=fn, h=h: fn(e).then_inc(h, 1))
        self._record((E["semname"], c, dict(E["seen"])), reads, writes)

    def dma(self, en, out_ap, in_ap, reads=(), writes=(), sem=None):
        E = self.eng[en]
        h = self.dsem(sem)
        self._apply_waits(en, self._collect(en, reads, writes, strict=True, own_sem=sem))
        self.dcount[sem] += 16
        c = self.dcount[sem]
        E["stream"].append(
            lambda e, h=h, o=out_ap, i=in_ap: e.dma_start(out=o, in_=i).then_inc(h, 16))
        self._record((sem, c, dict(E["seen"])), reads, writes)

    def barrier(self, waiters, on):
        for w in waiters:
            E = self.eng[w]
            need = {}
            for o in on:
                if o == w:
                    continue
                O = self.eng[o]
                if O["count"] > E["seen"].get(O["semname"], 0):
                    need[O["semname"]] = (O["count"], {})
            self._apply_waits(w, need)

    def wait_dma_final(self, en, semnames):
        need = {}
        for sn in semnames:
            if sn in self.dcount and self.dcount[sn] > 0:
                need[sn] = (self.dcount[sn], {})
        self._apply_waits(en, need)


def build(debug=False):
    nc = bass.Bass("TRN2", target_bir_lowering=False)
    xT_d = nc.dram_tensor("xT", [D, S], F32, kind="ExternalInput").ap()
    x_d = nc.dram_tensor("x", [S, D], F32, kind="ExternalInput").ap()
    win_d = nc.dram_tensor("w_in", [D, 10240], F32, kind="ExternalInput").ap()
    wa_d = nc.dram_tensor("w_a", [D, D], F32, kind="ExternalInput").ap()
    wb_d = nc.dram_tensor("w_b", [D, D], F32, kind="ExternalInput").ap()
    wo_d = nc.dram_tensor("w_o", [D, D], F32, kind="ExternalInput").ap()
    cst_d = nc.dram_tensor("cst", [128, NCST], F32, kind="ExternalInput").ap()
    lnp_d = nc.dram_tensor("lnp", [128, 2048], F32, kind="ExternalInput").ap()
    out_d = nc.dram_tensor("out", [S, D], F32, kind="ExternalOutput").ap()
    dbg = {}
    if debug:
        dbg["oT"] = nc.dram_tensor("dbg_oT", [128, 8 * S], BF16, kind="ExternalOutput").ap()
        dbg["cT"] = nc.dram_tensor("dbg_cT", [128, 8 * S], BF16, kind="ExternalOutput").ap()
        dbg["mT"] = nc.dram_tensor("dbg_mT", [128, 8 * S], BF16, kind="ExternalOutput").ap()

    win_v = win_d.rearrange("(dt p) c -> p dt c", p=128)
    wa_v = wa_d.rearrange("(dt p) c -> p dt c", p=128)
    wb_v = wb_d.rearrange("(dt p) c -> p dt c", p=128)
    wo_v = wo_d.rearrange("(dt p) c -> p dt c", p=128)

    cur = [(nc.sbuf_base + 63) // 64 * 64]
    top = nc.sbuf_top

    def sb(name, shape, dt):
        esz = 2 if dt == BF16 else 4
        nb = int(np.prod(shape[1:])) * esz
        a = cur[0]
        t = nc.alloc_sbuf_tensor_at(name, list(shape), dt, offset=a)
        cur[0] = (a + nb + 63) // 64 * 64
        assert cur[0] <= top, (name, cur[0], top)
        return t

    cst = sb("cst", [128, NCST], F32)
    ident = sb("ident", [128, 128], BF16)
    ones_bf = sb("ones_bf", [128, 128], BF16)
    mk = sb("mk", [128, 192], BF16)
    sm = sb("sm", [128, 128], F32)
    xT = sb("xTb", [128, 8, S], BF16)
    oT = sb("oT", [128, 8, S], BF16)
    wt = [sb("wt0", [128, 8, 512], BF16), sb("wt1", [128, 8, 512], BF16)]
    wtx = [sb("wtx0", [128, 8, 512], BF16), sb("wtx1", [128, 8, 512], BF16)]
    R0 = cur[0]
    qm = sb("qm", [128, 4, 2, S], BF16)
    kT = sb("kT", [128, 4, S], BF16)
    vv = sb("vv", [128, NT, 4, 128], BF16)
    gzT = sb("gzT", [128, 4, S], BF16)
    G = [sb("G%d" % i, [128, 512], F32) for i in range(7)]
    rq = [sb("rq%d" % i, [128, 512], BF16) for i in range(3)]
    Eb = [sb("E%d" % i, [128, 512], BF16) for i in range(4)]
    sqb = [sb("sq%d" % i, [128, 256], BF16) for i in range(2)]
    endA = cur[0]
    cur[0] = R0
    cT = sb("cT", [128, 8, S], BF16)
    mT = sb("mT", [128, 8, S], BF16)
    TT = sb("TT", [128, 8, 1032], F32)
    lnp = sb("lnp", [128, 2048], F32)
    endB = cur[0]
    assert max(endA, endB) <= top

    ps = nc.alloc_psum_tensor("ps", [128, 8, 512], F32)
    psb = ps[:, 6:8, :].bitcast(BF16)

    def tts(i):
        return TT[:, i // 2, (i % 2) * 516:(i % 2) * 516 + 516]

    SM_LS, SM_LE, SM_NL, SM_NH, SM_RL1, SM_RL2, SM_SS, SM_RSTD = 0, 2, 4, 8, 16, 20, 24, 28
    SM_BN, SM_MV, SM_RS2, SM_NB, SM_C2, SM_MB, SM_LE2, SM_NM = 32, 68, 74, 80, 84, 85, 86, 88

    stack = ExitStack()
    P = Prog(nc, stack)

    P.dma("sp", cst[:, :], cst_d[:, :], writes=["cst"], sem="d_cst")
    xT_v = xT_d.rearrange("(dt p) t -> p dt t", p=128)

    XR = [(0, 128), (128, 512), (512, 1024), (1024, 1536), (1536, 2048)]

    def xkeys(t0, t1):
        return [("xT", i) for i, (a, b_) in enumerate(XR) if a < t1 and b_ > t0]

    def xT_load(i):
        a, b_ = XR[i]
        P.dma("pool", xT[:, :, a:b_], xT_v[:, :, a:b_], writes=[("xT", i)], sem="d_xT%d" % i)

    P.op("dve", lambda e: e.tensor_copy(out=ident[:, :], in_=cst[:, C_ID:C_ID + 128]),
         reads=["cst"], writes=["ident"])
    P.op("dve", lambda e: e.memset(sm[:, SM_NH:SM_NH + 8], -0.5), writes=["nh"])
    P.op("dve", lambda e: e.memset(ones_bf[:, :], 1.0), writes=["ones"])
    P.op("dve", lambda e: e.memset(mk[0:1, 0:64], 0.0), writes=["mk0"])
    P.op("dve", lambda e: e.memset(mk[0:1, 64:128], 1.0), writes=["mk1"])
    P.op("dve", lambda e: e.memset(mk[0:1, 128:192], -30000.0), writes=["mk2"])
    P.op("dve", lambda e: e.memset(sm[:, SM_C2:SM_C2 + 1], RMS_EPS / (ONE_M_LAM * ONE_M_LAM)), writes=["c2"])
    P.op("dve", lambda e: e.memset(sm[:, SM_LE2:SM_LE2 + 1], LN_EPS), writes=["lneps"])
    P.op("dve", lambda e: e.memset(sm[0:64, SM_MB:SM_MB + 1], 0.0), writes=["mb0"])
    P.op("dve", lambda e: e.memset(sm[64:128, SM_MB:SM_MB + 1], -30000.0), writes=["mb1"])
    lam_a = bass.AP(cst, C_LAM, [[NCST, 128], [128, 2], [1, 64]])
    lam_b = bass.AP(cst, C_LAM + 64, [[NCST, 128], [128, 2], [1, 64]])
    lp = bass.AP(G[0], 0, [[512, 128], [64, 2], [1, 64]])
    P.op("dve", lambda e: e.tensor_tensor(out=lp, in0=lam_a, in1=lam_b, op=ALU.mult),
         reads=["cst"], writes=[("G", 0)])
    P.op("dve", lambda e: e.tensor_reduce(out=sm[:, SM_LS:SM_LS + 2], in_=lp, axis=AX.X, op=ALU.add),
         reads=[("G", 0)], writes=["ls"])
    P.op("act", lambda e: e.activation(out=sm[:, SM_LE:SM_LE + 2], in_=sm[:, SM_LS:SM_LS + 2], func=AF.Exp),
         reads=["ls"], writes=["le"])
    P.op("dve", lambda e: e.tensor_tensor(out=sm[:, SM_NL:SM_NL + 1], in0=sm[:, SM_LE + 1:SM_LE + 2],
                                          in1=sm[:, SM_LE:SM_LE + 1], op=ALU.subtract),
         reads=["le"], writes=["nl0"])
    P.op("dve", lambda e: e.tensor_scalar(out=sm[:, SM_NL + 1:SM_NL + 2], in0=sm[:, SM_NL:SM_NL + 1],
                                          scalar1=-LAM_INIT, scalar2=None, op0=ALU.add),
         reads=["nl0"], writes=["nlam"])
    nlam = sm[:, SM_NL + 1:SM_NL + 2]

    loads = []

    def ld_p1(g, seg, b):
        def f():
            c0 = seg * 1024 + g * 512
            P.dma("pool", wt[b][:, :, :], win_v[:, :, c0:c0 + 512], writes=[("wt", b)], sem="d_wt%d" % b)
        return f

    def ld_p2(cp, b):
        def f():
            for s_ in range(4):
                c0 = 4096 + s_ * 1024 + cp * 256
                dst = (wt[b] if s_ < 2 else wtx[b])[:, :, (s_ % 2) * 256:(s_ % 2) * 256 + 256]
                key, sem = (("wt", b), "d_wt%d" % b) if s_ < 2 else (("wtx", b), "d_wtx%d" % b)
                P.dma("pool", dst, win_v[:, :, c0:c0 + 256], writes=[key], sem=sem)
        return f

    def ld_p4(qd, b):
        def f():
            c0 = qd * 256
            P.dma("pool", wt[b][:, :, 0:256], wa_v[:, :, c0:c0 + 256], writes=[("wt", b)], sem="d_wt%d" % b)
            P.dma("pool", wt[b][:, :, 256:512], wb_v[:, :, c0:c0 + 256], writes=[("wt", b)], sem="d_wt%d" % b)
            P.dma("pool", wtx[b][:, :, 0:256], win_v[:, :, 8192 + c0:8192 + c0 + 256],
                  writes=[("wtx", b)], sem="d_wtx%d" % b)
            P.dma("pool", wtx[b][:, :, 256:512], win_v[:, :, 9216 + c0:9216 + c0 + 256],
                  writes=[("wtx", b)], sem="d_wtx%d" % b)
        return f

    def ld_wo():
        P.dma("pool", wt[0][:, :, :], wo_v[:, :, 0:512], writes=[("wt", 0)], sem="d_wt0")
        P.dma("pool", wtx[0][:, :, :], wo_v[:, :, 512:1024], writes=[("wtx", 0)], sem="d_wtx0")

    for g in range(2):
        for seg in range(4):
            loads.append(ld_p1(g, seg, seg % 2))
    for cp in range(4):
        loads.append(ld_p2(cp, cp % 2))
    for qd in range(4):
        loads.append(ld_p4(qd, qd % 2))
    loads.append(ld_wo)
    nload = [0]

    pf_limit = [99]

    def prefetch(upto):
        upto = min(upto, pf_limit[0])
        while nload[0] <= upto and nload[0] < len(loads):
            loads[nload[0]]()
            nload[0] += 1

    prefetch(0)
    xT_load(0)
    xT_load(1)
    prefetch(1)
    for i_ in range(2, 5):
        xT_load(i_)
    P.op("pool", lambda e: e.memset(qm[64:128, :, 0, :], 0.0), writes=["qz0"])
    P.op("pool", lambda e: e.memset(qm[0:64, :, 1, :], 0.0), writes=["qz1"])
    ucnt = [0]

    pend1 = []

    def flush1(keep=0):
        while len(pend1) > keep:
            pend1.pop(0)()

    def p1_unit(g, seg, b, t):
        u = ucnt[0]
        ucnt[0] += 1
        bank = u % 2
        p = u % 2
        p3 = u % 3
        pst = ps[:, bank, :]
        fns = []
        for dt_ in range(8):
            fns.append(lambda e, dt_=dt_: e.matmul(
                pst, xT[:, dt_, t * 128:(t + 1) * 128], wt[b][:, dt_, :],
                start=(dt_ == 0), stop=(dt_ == 7)))
        P.group("pe", fns, reads=xkeys(t * 128, t * 128 + 128) + [("wt", b)], writes=[("ps", bank)])
        flush1(keep=1)
        if seg < 2:
            p4 = bass.AP(ps, bank * 512, [[4096, 128], [64, 8], [32, 2], [1, 32]])
            cosb = bass.AP(cst, C_COS + t * 32, [[NCST, 128], [0, 8], [0, 2], [1, 32]])
            sinb = bass.AP(cst, C_SIN + t * 32, [[NCST, 128], [0, 8], [1, 32]])
            a4 = bass.AP(G[p], 0, [[512, 128], [64, 8], [32, 2], [1, 32]])
            P.op("dve", lambda e: e.tensor_tensor(out=a4, in0=p4, in1=cosb, op=ALU.mult),
                 reads=[("ps", bank), "cst"], writes=[("G", p)])
            pt2 = bass.AP(ps, bank * 512 + 32, [[4096, 128], [64, 8], [1, 32]])
            pt1 = bass.AP(ps, bank * 512, [[4096, 128], [64, 8], [1, 32]])
            b1 = bass.AP(G[2 + p], 0, [[512, 128], [32, 8], [1, 32]])
            b2 = bass.AP(G[2 + p], 256, [[512, 128], [32, 8], [1, 32]])
            P.op("dve", lambda e: e.tensor_tensor(out=b1, in0=pt2, in1=sinb, op=ALU.mult),
                 reads=[("ps", bank), "cst"], writes=[("Ta", p)])
            P.op("dve", lambda e: e.tensor_tensor(out=b2, in0=pt1, in1=sinb, op=ALU.mult),
                 reads=[("ps", bank), "cst"], writes=[("Tb", p)])
            a1 = bass.AP(G[p], 0, [[512, 128], [64, 8], [1, 32]])
            a2 = bass.AP(G[p], 32, [[512, 128], [64, 8], [1, 32]])
            r1 = bass.AP(rq[p3], 0, [[512, 128], [64, 8], [1, 32]])
            r2 = bass.AP(rq[p3], 32, [[512, 128], [64, 8], [1, 32]])
            P.op("pool", lambda e: e.tensor_tensor(out=r1, in0=a1, in1=b1, op=ALU.subtract),
                 reads=[("G", p), ("Ta", p)], writes=[("rq", p3)])
            P.op("pool", lambda e: e.tensor_tensor(out=r2, in0=a2, in1=b2, op=ALU.add),
                 reads=[("G", p), ("Tb", p)], writes=[("rq", p3)])
            tb = u % 2

            def later():
                fns2 = []
                for hh in range(4):
                    fns2.append(lambda e, hh=hh: e.transpose(
                        psb[:, tb, hh * 128:(hh + 1) * 128], rq[p3][:, hh * 128:(hh + 1) * 128], ident[:, :]))
                P.group("pe", fns2, reads=[("rq", p3), "ident"], writes=[("ps", 6 + tb)])
                if seg == 0:
                    for c in range(2):
                        srcv = psb[c * 64:(c + 1) * 64, tb, 0:512].rearrange("p (h q) -> p h q", h=4)
                        dstv = qm[c * 64:(c + 1) * 64, :, c, t * 128:(t + 1) * 128]
                        P.op("act", lambda e, srcv=srcv, dstv=dstv: e.activation(out=dstv, in_=srcv, func=AF.Copy),
                             reads=[("ps", 6 + tb), "qz0", "qz1"], writes=[("qT", t, c)])
                else:
                    srcv = psb[:, tb, 0:512].rearrange("p (h q) -> p h q", h=4)
                    P.op("act", lambda e: e.activation(
                        out=kT[:, :, t * 128:(t + 1) * 128], in_=srcv, func=AF.Copy),
                        reads=[("ps", 6 + tb)], writes=[("kT", t)])
            pend1.append(later)
        else:
            P.op("act", lambda e: e.activation(out=vv[:, t, :, :].rearrange("p h e -> p (h e)"), in_=pst, func=AF.Copy),
                 reads=[("ps", bank)], writes=[("v", t)])

    def p1z_unit(g, b, hh, tc):
        u = ucnt[0]
        ucnt[0] += 1
        bank = u % 2
        p = u % 2
        pst = ps[:, bank, :]
        fns = []
        for dt_ in range(8):
            fns.append(lambda e, dt_=dt_: e.matmul(
                pst, wt[b][:, dt_, hh * 128:(hh + 1) * 128], xT[:, dt_, tc * 512:(tc + 1) * 512],
                start=(dt_ == 0), stop=(dt_ == 7)))
        P.group("pe", fns, reads=xkeys(tc * 512, tc * 512 + 512) + [("wt", b)], writes=[("ps", bank)])
        flush1(keep=1)
        P.op("act", lambda e: e.activation(out=G[4 + p][:, :], in_=pst, func=AF.Silu),
             reads=[("ps", bank)], writes=[("G4a", p), ("G4b", p)])
        P.op("pool", lambda e: e.tensor_scalar(
            out=gzT[:, hh, tc * 512:(tc + 1) * 512], in0=G[4 + p][:, :], scalar1=cst[:, C_GC:C_GC + 1], scalar2=1.0,
            op0=ALU.mult, op1=ALU.mult),
            reads=[("G4a", p), ("G4b", p), "cst"], writes=[("gz", hh, tc)])

    def phase1(g, li0):
        for seg in range(4):
            prefetch(li0 + seg + 1)
            if seg < 3:
                for t in range(NT):
                    p1_unit(g, seg, seg % 2, t)
            else:
                for hh in range(4):
                    for tc in range(4):
                        p1z_unit(g, seg % 2, hh, tc)
        flush1()

    def phase3(g):
        steps = []
        for hh in range(4):
            for hc in (0, 7, 1, 6, 2, 5, 3, 4):
                for j in range(2 * hc + 2):
                    steps.append((hh, hc, j))
        n = len(steps)
        pend = []
        unit_of = {}
        sctr = [0]

        def next_sbank():
            bk = (0, 1, 6, 7)[sctr[0] % 4]
            sctr[0] += 1
            return bk

        def emit_S(i):
            hh, hc, j = steps[i]
            sbk = next_sbank()
            eb = i % 4
            r = max(0, j - 2 * hc)
            N = 256 - 128 * r
            q0 = 256 * hc + 128 * r
            so = bass.AP(ps, sbk * 512 + 128 * r, [[4096, 128], [256, 2], [1, N]])
            diag = j >= 2 * hc
            if r == 0:
                sfns = [lambda e: e.matmul(
                    so, kT[:, hh, j * 128:(j + 1) * 128], qm[:, hh, :, q0:q0 + N], start=True, stop=True)]
            else:
                sfns = [lambda e, c=c: e.matmul(
                    ps[:, sbk, c * 256 + 128:c * 256 + 256], kT[:, hh, j * 128:(j + 1) * 128], qm[:, hh, c, q0:q0 + N],
                    start=(c == 0), stop=(c == 1), skip_group_check=True) for c in range(2)]
            P.group("pe", sfns,
                reads=[("kT", j), "qz0", "qz1"] + [("qT", 2 * hc + qt, c) for qt in range(r, 2) for c in range(2)],
                writes=[("ps", sbk)])
            ev = bass.AP(Eb[eb], 128 * r, [[512, 128], [256, 2], [1, N]])
            P.op("act", lambda e: e.activation(out=ev, in_=so, func=AF.Exp, scale=0.125),
                 reads=[("ps", sbk)], writes=[("E", eb)])
            if diag:
                mv = bass.AP(Eb[eb], 64 * 512 + 128 * r, [[512, 64], [256, 2], [1, 64]])
                P.op("dve", lambda e: e.memset(mv, 0.0), reads=[], writes=[("E", eb)])

        def emit_AV(i):
            hh, hc, j = steps[i]
            if (hh, hc) not in unit_of:
                unit_of[(hh, hc)] = ucnt[0]
                ucnt[0] += 1
            u = unit_of[(hh, hc)]
            p = u % 2
            eb = i % 4
            r = max(0, j - 2 * hc)
            N = 256 - 128 * r
            ev = bass.AP(Eb[eb], 128 * r, [[512, 128], [256, 2], [1, N]])
            oo = bass.AP(ps, (2 + 2 * p) * 512 + 128 * r, [[4096, 128], [256, 2], [1, N]])
            lo = bass.AP(ps, (3 + 2 * p) * 512 + 128 * r, [[4096, 128], [256, 2], [1, N]])
            st, sp_ = (j == 0), (j == 2 * hc + 1)
            if r == 0:
                afns = [
                    lambda e: e.matmul(oo, vv[:, j, hh, :], ev, start=st, stop=sp_, skip_group_check=True),
                    lambda e: e.matmul(lo, ones_bf[:, :], ev, start=st, stop=sp_, skip_group_check=True)]
            else:
                afns = []
                for c in range(2):
                    cs = slice(c * 256 + 128, c * 256 + 256)
                    afns.append(lambda e, cs=cs: e.matmul(ps[:, 2 + 2 * p, cs], vv[:, j, hh, :], Eb[eb][:, cs],
                                                          start=False, stop=sp_, skip_group_check=True))
                    afns.append(lambda e, cs=cs: e.matmul(ps[:, 3 + 2 * p, cs], ones_bf[:, :], Eb[eb][:, cs],
                                                          start=False, stop=sp_, skip_group_check=True))
            P.group("pe", afns,
                reads=[("E", eb), ("v", j), "ones"], writes=[("ps", 2 + 2 * p), ("ps", 3 + 2 * p)])
            if sp_:
                post(i, hh, hc, u)

        def post(i, hh, hc, u):
            p = u % 2
            ru = u % 2
            hg = 4 * g + hh
            ob_, lb_ = 2 + 2 * p, 3 + 2 * p
            Ls = G[ru]
            T = G[2 + ru]
            o_ = G[4 + ru][:, 0:256]
            on_ = G[4 + ru][:, 256:512]
            rs_ = G[6][:, ru * 256:(ru + 1) * 256]
            ko, kn, kr = ("G4a", ru), ("G4b", ru), ("G", 6, ru)
            P.op("dve", lambda e: e.tensor_copy(out=Ls[:, :], in_=ps[:, lb_, :]),
                 reads=[("ps", lb_)], writes=[("G", ru)], strict=True)
            P.op("dve", lambda e: e.tensor_tensor(out=T[:, 0:256], in0=ps[:, ob_, 0:256], in1=Ls[:, 256:512], op=ALU.mult),
                 reads=[("ps", ob_), ("G", ru)], writes=[("Ta", ru)], strict=True)
            P.op("dve", lambda e: e.tensor_tensor(out=T[:, 256:512], in0=ps[:, ob_, 256:512], in1=Ls[:, 0:256], op=ALU.mult),
                 reads=[("ps", ob_), ("G", ru)], writes=[("Tb", ru)], strict=True)
            P.op("dve", lambda e: e.scalar_tensor_tensor(out=o_, in0=T[:, 256:512], scalar=nlam, in1=T[:, 0:256],
                                                         op0=ALU.mult, op1=ALU.add),
                 reads=[("Ta", ru), ("Tb", ru), "nlam"], writes=[ko], strict=True)
            P.op("pool", lambda e: e.tensor_tensor(out=sqb[ru][:, :], in0=o_, in1=o_, op=ALU.mult),
                 reads=[ko], writes=[("sq", ru)], strict=True)
            c1 = 1.0 / (128.0 * ONE_M_LAM * ONE_M_LAM)
            c2 = RMS_EPS / (ONE_M_LAM * ONE_M_LAM)
            P.op("dve", lambda e: e.tensor_tensor(out=on_, in0=Ls[:, 0:256], in1=Ls[:, 256:512], op=ALU.mult),
                 reads=[("G", ru)], writes=[kn], strict=True)
            P.op("dve", lambda e: e.scalar_tensor_tensor(out=rs_, in0=on_, scalar=c2, in1=on_,
                                                         op0=ALU.mult, op1=ALU.mult),
                 reads=[kn], writes=[kr], strict=True)
            q0 = 256 * hc

            rb = {}

            def later1():
                rbk = next_sbank()
                rb["bk"] = rbk
                ssp = ps[:, rbk, 0:256]
                P.group("pe", [lambda e: e.matmul(ssp, ones_bf[:, :], sqb[ru][:, :], start=True, stop=True, skip_group_check=True)],
                        reads=[("sq", ru), "ones"], writes=[("ps", rbk)])

            def later2():
                rbk = rb["bk"]
                ssp = ps[:, rbk, 0:256]
                P.op("dve", lambda e: e.scalar_tensor_tensor(out=rs_, in0=ssp, scalar=c1, in1=rs_,
                                                             op0=ALU.mult, op1=ALU.add),
                     reads=[("ps", rbk), kr], writes=[kr], strict=True)
                P.op("act", lambda e: e.activation(out=rs_, in_=rs_, func=AF.Ln),
                     reads=[kr], writes=[kr], strict=True)
                P.op("act", lambda e: e.activation(out=rs_, in_=rs_, func=AF.Exp, scale=-0.5),
                     reads=[kr], writes=[kr], strict=True)
                P.op("pool", lambda e: e.tensor_tensor(out=on_, in0=o_, in1=rs_, op=ALU.mult),
                     reads=[ko, kr], writes=[kn], strict=True)
                P.op("pool", lambda e: e.tensor_tensor(out=oT[:, hg, q0:q0 + 256], in0=on_, in1=gzT[:, hh, q0:q0 + 256], op=ALU.mult),
                     reads=[kn] + [("gz", hh, tc) for tc in range(4)], writes=[("oT", hg)], strict=True)
            pend.append((i + 9, later1))
            pend.append((i + 11, later2))

        def flush(upto):
            k = 0
            while k < len(pend):
                if pend[k][0] <= upto:
                    pend.pop(k)[1]()
                else:
                    k += 1

        emit_S(0)
        emit_S(1)
        emit_S(2)
        for i in range(n):
            if i + 3 < n:
                emit_S(i + 3)
            emit_AV(i)
            flush(i)
        flush(10 ** 9)

    def p2_unit(b, ct, sub, tc):
        u = ucnt[0]
        ucnt[0] += 1
        pset = (u % 2) * 4
        tset = (u % 2) * 5
        for s_ in range(4):
            wsrc = wt[b] if s_ < 2 else wtx[b]
            wkey = ("wt", b) if s_ < 2 else ("wtx", b)
            c0 = (s_ % 2) * 256 + sub * 128
            fns = []
            for dt_ in range(8):
                fns.append(lambda e, dt_=dt_, wsrc=wsrc, c0=c0, s_=s_: e.matmul(
                    ps[:, pset + s_, :], wsrc[:, dt_, c0:c0 + 128], xT[:, dt_, tc * 512:(tc + 1) * 512],
                    start=(dt_ == 0), stop=(dt_ == 7)))
            P.group("pe", fns, reads=xkeys(tc * 512, tc * 512 + 512) + [wkey], writes=[("ps", pset + s_)])
        hs, ut, acc, sz, tm = [tts(tset + k) for k in range(5)]
        kh, ku, ka, ks, kt = [("TT", tset + k) for k in range(5)]
        P.op("act", lambda e: e.activation(out=hs[:, 0:512], in_=ps[:, pset + 0, :], func=AF.Copy),
             reads=[("ps", pset)], writes=[kh])
        if tc == 0:
            P.op("pool", lambda e: e.memset(ut[:, 0:2], 0.0), writes=[ku])
        else:
            pi = ((u - 1) % 2) * 5 + 1
            prev = tts(pi)
            P.op("pool", lambda e: e.tensor_copy(out=ut[:, 0:2], in_=prev[:, 512:514]),
                 reads=[("TT", pi)], writes=[ku])
        P.op("dve", lambda e: e.tensor_tensor(out=ut[:, 2:514], in0=ps[:, pset + 2, :], in1=hs[:, 0:512], op=ALU.mult),
             reads=[("ps", pset + 2), kh, ku], writes=[ku])
        cw = [cst[:, C_CW + ct * 3 + k:C_CW + ct * 3 + k + 1] for k in range(3)]
        cb = cst[:, C_CB + ct:C_CB + ct + 1]
        P.op("dve", lambda e: e.tensor_scalar(out=acc[:, 0:512], in0=ut[:, 2:514], scalar1=cw[2], scalar2=cb,
                                              op0=ALU.mult, op1=ALU.add),
             reads=[ku, "cst"], writes=[ka])
        P.op("dve", lambda e: e.scalar_tensor_tensor(out=acc[:, 0:512], in0=ut[:, 1:513], scalar=cw[1], in1=acc[:, 0:512],
                                                     op0=ALU.mult, op1=ALU.add),
             reads=[ku, ka, "cst"], writes=[ka])
        P.op("dve", lambda e: e.scalar_tensor_tensor(out=acc[:, 0:512], in0=ut[:, 0:512], scalar=cw[0], in1=acc[:, 0:512],
                                                     op0=ALU.mult, op1=ALU.add),
             reads=[ku, ka, "cst"], writes=[ka])
        P.op("act", lambda e: e.activation(out=sz[:, 0:512], in_=ps[:, pset + 3, :], func=AF.Silu),
             reads=[("ps", pset + 3)], writes=[ks])
        P.op("dve", lambda e: e.tensor_tensor(out=tm[:, 0:512], in0=ps[:, pset + 1, :], in1=acc[:, 0:512], op=ALU.mult),
             reads=[("ps", pset + 1), ka], writes=[kt])
        P.op("pool", lambda e: e.tensor_tensor(out=cT[:, ct, tc * 512:(tc + 1) * 512], in0=tm[:, 0:512], in1=sz[:, 0:512], op=ALU.mult),
             reads=[kt, ks], writes=[("cT", ct)])

    def phase2(li0):
        for cp in range(4):
            prefetch(li0 + cp + 1)
            for sub in range(2):
                for tc in range(4):
                    p2_unit(cp % 2, 2 * cp + sub, sub, tc)

    def p4a_unit(b, dt_o, sub, tc):
        u = ucnt[0]
        ucnt[0] += 1
        pset = (u % 2) * 4
        tset = (u % 2) * 5
        specs = [(wt[b], ("wt", b), 0, oT, [("oT", k) for k in range(8)]),
                 (wt[b], ("wt", b), 256, cT, [("cT", k) for k in range(8)]),
                 (wtx[b], ("wtx", b), 0, xT, xkeys(tc * 512, tc * 512 + 512)),
                 (wtx[b], ("wtx", b), 256, xT, xkeys(tc * 512, tc * 512 + 512))]
        for s_, (wsrc, wkey, cb0, src_, skeys) in enumerate(specs):
            c0 = cb0 + sub * 128
            fns = []
            for k in range(8):
                fns.append(lambda e, k=k, wsrc=wsrc, c0=c0, s_=s_, src_=src_: e.matmul(
                    ps[:, pset + s_, :], wsrc[:, k, c0:c0 + 128], src_[:, k, tc * 512:(tc + 1) * 512],
                    start=(k == 0), stop=(k == 7)))
            P.group("pe", fns, reads=skeys + [wkey], writes=[("ps", pset + s_)])
        sa, sb_, m1, m2 = [tts(tset + k) for k in range(4)]
        ksa, ksb, km1, km2 = [("TT", tset + k) for k in range(4)]
        bga = cst[:, C_BG + dt_o:C_BG + dt_o + 1]
        bgb = cst[:, C_BG + 8 + dt_o:C_BG + 8 + dt_o + 1]
        P.op("act", lambda e: e.activation(out=sa[:, 0:512], in_=ps[:, pset + 2, :], func=AF.Sigmoid, bias=bga),
             reads=[("ps", pset + 2), "cst"], writes=[ksa])
        P.op("act", lambda e: e.activation(out=sb_[:, 0:512], in_=ps[:, pset + 3, :], func=AF.Sigmoid, bias=bgb),
             reads=[("ps", pset + 3), "cst"], writes=[ksb])
        P.op("dve", lambda e: e.tensor_tensor(out=m1[:, 0:512], in0=ps[:, pset + 0, :], in1=sa[:, 0:512], op=ALU.mult),
             reads=[("ps", pset + 0), ksa], writes=[km1])
        P.op("dve", lambda e: e.tensor_tensor(out=m2[:, 0:512], in0=ps[:, pset + 1, :], in1=sb_[:, 0:512], op=ALU.mult),
             reads=[("ps", pset + 1), ksb], writes=[km2])
        P.op("pool", lambda e: e.tensor_tensor(out=mT[:, dt_o, tc * 512:(tc + 1) * 512], in0=m1[:, 0:512], in1=m2[:, 0:512], op=ALU.add),
             reads=[km1, km2], writes=[("mT", dt_o)])

    def phase4a(li0):
        for qd in range(4):
            prefetch(li0 + qd + 1)
            for sub in range(2):
                for tc in range(4):
                    p4a_unit(qd % 2, 2 * qd + sub, sub, tc)

    mkeys = [("mT", k) for k in range(8)]

    def xload(t):
        k = t % 3
        P.dma("sp", TT[:, k, 0:1024], x_d[t * 128:(t + 1) * 128, :],
              writes=[("TT", 2 * k), ("TT", 2 * k + 1)], sem="d_x%d" % k)

    def p4b_A(t):
        k = 3 + (t % 5)
        kx = t % 3
        xk = [("TT", 2 * kx), ("TT", 2 * kx + 1)]
        pb = (t % 4) * 2
        for n_ in range(2):
            wsrc = wt[0] if n_ == 0 else wtx[0]
            wkey = ("wt", 0) if n_ == 0 else ("wtx", 0)
            fns = []
            for kk in range(8):
                fns.append(lambda e, kk=kk, wsrc=wsrc, n_=n_: e.matmul(
                    ps[:, pb + n_, :], mT[:, kk, t * 128:(t + 1) * 128], wsrc[:, kk, :],
                    start=(kk == 0), stop=(kk == 7)))
            P.group("pe", fns, reads=mkeys + [wkey], writes=[("ps", pb + n_)])
        yk = [("TT", 2 * k), ("TT", 2 * k + 1)]
        y = TT[:, k, 0:1024]
        psv = bass.AP(ps, pb * 512, [[4096, 128], [1, 1024]])
        P.op("dve", lambda e: e.scalar_tensor_tensor(out=y, in0=TT[:, kx, 0:1024], scalar=DN_ALPHA, in1=psv,
                                                     op0=ALU.mult, op1=ALU.add),
             reads=xk + [("ps", pb), ("ps", pb + 1)], writes=yk)
        s2 = t % 3
        so = SM_BN + s2 * 12
        kb = ("bn", s2)
        P.op("dve", lambda e: e.bn_stats(out=sm[:, so:so + 6], in_=TT[:, k, 0:512]), reads=yk, writes=[kb])
        P.op("dve", lambda e: e.bn_stats(out=sm[:, so + 6:so + 12], in_=TT[:, k, 512:1024]), reads=yk, writes=[kb])
        mo = SM_MV + s2 * 2
        P.op("dve", lambda e: e.bn_aggr(out=sm[:, mo:mo + 2], in_=sm[:, so:so + 12]),
             reads=[kb], writes=[("mv", s2)])
        ro = SM_RS2 + s2 * 2
        P.op("dve", lambda e: e.tensor_scalar(out=sm[:, SM_NM + s2:SM_NM + s2 + 1], in0=sm[:, mo:mo + 1], scalar1=-1.0, scalar2=None, op0=ALU.mult),
             reads=[("mv", s2)], writes=[("nm", s2)])
        P.op("act", lambda e: e.activation(out=sm[:, ro:ro + 1], in_=sm[:, mo + 1:mo + 2], func=AF.Ln, bias=sm[:, SM_LE2:SM_LE2 + 1]),
             reads=[("mv", s2), "lneps"], writes=[("ve", s2)])
        P.op("act", lambda e: e.activation(out=sm[:, ro + 1:ro + 2], in_=sm[:, ro:ro + 1], func=AF.Exp, scale=-0.5),
             reads=[("ve", s2)], writes=[("rs", s2)])

    def p4b_B(t):
        k = 3 + (t % 5)
        s2 = t % 3
        yk = [("TT", 2 * k), ("TT", 2 * k + 1)]
        y = TT[:, k, 0:1024]
        mo = SM_MV + s2 * 2
        ro = SM_RS2 + s2 * 2
        no = SM_NB + s2
        P.op("act", lambda e: e.activation(out=sm[:, no:no + 1], in_=sm[:, SM_NM + s2:SM_NM + s2 + 1], func=AF.Copy, scale=sm[:, ro + 1:ro + 2]),
             reads=[("nm", s2), ("rs", s2)], writes=[("nb", s2)])
        P.op("act", lambda e: e.activation(out=y, in_=y, func=AF.Identity, scale=sm[:, ro + 1:ro + 2], bias=sm[:, no:no + 1]),
             reads=yk + [("rs", s2), ("nb", s2)], writes=yk)

    def p4b_C(t):
        k = 3 + (t % 5)
        yk = [("TT", 2 * k), ("TT", 2 * k + 1)]
        y = TT[:, k, 0:1024]
        P.op("dve", lambda e: e.tensor_tensor(out=y, in0=y, in1=lnp[:, 0:1024], op=ALU.mult),
             reads=yk + ["lnp"], writes=yk)
        P.op("pool", lambda e: e.tensor_tensor(out=y, in0=y, in1=lnp[:, 1024:2048], op=ALU.add),
             reads=yk + ["lnp"], writes=yk)
        P.dma("sp", out_d[t * 128:(t + 1) * 128, :], y, reads=yk, sem="d_o%d" % (k - 3))

    def phase4b():
        for t in range(2):
            xload(t)
        for t in range(NT + 2):
            if t + 2 < NT:
                xload(t + 2)
            if t < NT:
                p4b_A(t)
            if 1 <= t <= NT:
                p4b_B(t - 1)
            if 2 <= t <= NT + 1:
                p4b_C(t - 2)

    phase1(0, 0)
    phase3(0)
    phase1(1, 4)
    phase3(1)
    if debug:
        P.dma("sp", dbg["oT"], oT[:, :, :].rearrange("p a b -> p (a b)"), reads=[("oT", k) for k in range(8)], sem="d_dbg0")
    P.barrier(["act", "dve", "pool", "sp"], ["pe", "act", "dve", "pool"])
    P.dma("sp", lnp[:, :], lnp_d[:, :], writes=["lnp"], sem="d_lnp")
    phase2(8)
    phase4a(12)
    if debug:
        P.dma("sp", dbg["cT"], cT[:, :, :].rearrange("p a b -> p (a b)"), reads=[("cT", k) for k in range(8)], sem="d_dbg1")
        P.dma("sp", dbg["mT"], mT[:, :, :].rearrange("p a b -> p (a b)"), reads=[("mT", k) for k in range(8)], sem="d_dbg2")
    phase4b()
    P.wait_dma_final("sp", ["d_o0", "d_o1", "d_o2", "d_o3", "d_o4", "d_dbg0", "d_dbg1", "d_dbg2"])

    with nc.allow_low_precision("bf16 matmul operands, fp32 accumulation"):
        with nc.Block() as block:
            @block.tensor
            def _(e):
                for f in P.eng["pe"]["stream"]:
                    f(e)

            @block.scalar
            def _(e):
                for f in P.eng["act"]["stream"]:
                    f(e)

            @block.vector
            def _(e):
                for f in P.eng["dve"]["stream"]:
                    f(e)

            @block.gpsimd
            def _(e):
                for f in P.eng["pool"]["stream"]:
                    f(e)

            @block.sync
            def _(e):
                for f in P.eng["sp"]["stream"]:
                    f(e)
    stack.close()
    return nc


def make_consts(inputs):
    cst = np.zeros((128, NCST), np.float32)
    half = 32
    inv_freq = (1.0 / (np.float32(10000.0) ** (np.arange(half, dtype=np.float32) / np.float32(half)))).astype(np.float32)
    pos = np.arange(S, dtype=np.float32)
    ang = (pos[:, None] * inv_freq[None, :]).astype(np.float32)
    cos = np.cos(ang).astype(np.float32).reshape(NT, 128, half).transpose(1, 0, 2).reshape(128, NT * half)
    sin = np.sin(ang).astype(np.float32).reshape(NT, 128, half).transpose(1, 0, 2).reshape(128, NT * half)
    cst[:, C_COS:C_COS + 512] = cos
    cst[:, C_SIN:C_SIN + 512] = sin
    cst[:, C_GC] = inputs["subln_g"][0]
    cst[:, C_BG:C_BG + 16] = inputs["b_gate"][0].reshape(16, 128).T
    cst[:, C_CW:C_CW + 24] = inputs["conv_w"][0].reshape(3, 8, 128).transpose(2, 1, 0).reshape(128, 24)
    cst[:, C_CB:C_CB + 8] = inputs["conv_b"][0].reshape(8, 128).T
    lam = np.concatenate([inputs["lambda_q1"][0], inputs["lambda_k1"][0],
                          inputs["lambda_q2"][0], inputs["lambda_k2"][0]])
    cst[:, C_LAM:C_LAM + 256] = lam[None, :]
    cst[:, C_ID:C_ID + 128] = np.eye(128, dtype=np.float32)
    return cst


_NC_CACHE = {}


def kernel(**inputs):
    inputs = {k: np.asarray(v) for k, v in inputs.items()}
    x = inputs["x"].astype(np.float32, copy=False)
    B = x.shape[0]
    if "nc" not in _NC_CACHE:
        _NC_CACHE["nc"] = build()
    nc = _NC_CACHE["nc"]
    cst = make_consts(inputs)
    lnp = np.ascontiguousarray(np.broadcast_to(
        np.concatenate([inputs["ln_g"][0], inputs["ln_b"][0]]).astype(np.float32)[None, :], (128, 2048)))
    w_in = np.ascontiguousarray(inputs["w_in"][0], dtype=np.float32)
    w_a = np.ascontiguousarray(inputs["w_a_out"][0], dtype=np.float32)
    w_b = np.ascontiguousarray(inputs["w_b_out"][0], dtype=np.float32)
    w_o = np.ascontiguousarray(inputs["w_o"][0], dtype=np.float32)
    in_maps = []
    for b in range(B):
        in_maps.append({
            "xT": np.ascontiguousarray(x[b].T),
            "x": np.ascontiguousarray(x[b]),
            "w_in": w_in, "w_a": w_a, "w_b": w_b, "w_o": w_o, "cst": cst, "lnp": lnp,
        })
    res = run_bass_kernel_spmd(nc, in_maps, core_ids=list(range(B)))
    return np.stack([np.asarray(r["out"]) for r in res.results], axis=0).astype(np.float32)
```

```python
import math
from contextlib import ExitStack

import numpy as np
import concourse.bass as bass
import concourse.mybir as mybir
from concourse.bass_utils import run_bass_kernel_spmd

F32 = mybir.dt.float32
BF16 = mybir.dt.bfloat16
AF = mybir.ActivationFunctionType
ALU = mybir.AluOpType
AX = mybir.AxisListType

S = 2048
D = 1024
NT = 16
LN_EPS = 1e-5
RMS_EPS = 1e-5
DN_ALPHA = 2.0 ** 0.25
LAM_INIT = 0.8 - 0.6 * math.exp(-0.3 * 0)
ONE_M_LAM = 1.0 - LAM_INIT

C_COS, C_SIN = 0, 512
C_BG, C_CW, C_CB, C_LAM, C_ID, C_GC = 1024, 1040, 1064, 1072, 1328, 1456
NCST = 1460


class Prog:
    def __init__(self, nc, stack):
        self.nc = nc
        self.stack = stack
        self.eng = {}
        self.sems = {}
        for n in ("pe", "act", "dve", "pool", "sp"):
            h = stack.enter_context(nc.semaphore("s_" + n))
            self.eng[n] = dict(sem=h, semname="s_" + n, count=0, seen={}, stream=[])
            self.sems["s_" + n] = h
        self.dcount = {}
        self.res = {}

    def dsem(self, name):
        if name not in self.sems:
            self.sems[name] = self.stack.enter_context(self.nc.semaphore(name))
            self.dcount[name] = 0
        return self.sems[name]

    def _collect(self, en, reads, writes, strict=False, own_sem=None):
        E = self.eng[en]
        need = {}

        def req(tok, raw):
            if tok is None:
                return
            sn, c, snap = tok
            if sn == own_sem:
                return
            if sn == E["semname"] and not strict:
                if (not raw) or en == "pe":
                    return
            if E["seen"].get(sn, 0) >= c:
                return
            if need.get(sn, (0, None))[0] < c:
                need[sn] = (c, snap)

        for k in reads:
            r = self.res.get(k)
            if r:
                req(r["w"], True)
        for k in writes:
            r = self.res.get(k)
            if r:
                req(r["w"], False)
                for tok in r["r"].values():
                    req(tok, False)
        return need

    def _apply_waits(self, en, need):
        E = self.eng[en]
        for sn, (c, snap) in need.items():
            h = self.sems[sn]
            E["stream"].append(lambda e, h=h, c=c: e.wait_ge(h, c))
            if E["seen"].get(sn, 0) < c:
                E["seen"][sn] = c
            for k2, v2 in snap.items():
                if E["seen"].get(k2, 0) < v2:
                    E["seen"][k2] = v2

    def _record(self, tok, reads, writes):
        sn = tok[0]
        for k in reads:
            r = self.res.setdefault(k, dict(w=None, r={}))
            r["r"][sn] = tok
        for k in writes:
            self.res[k] = dict(w=tok, r={})

    def op(self, en, fn, reads=(), writes=(), strict=False):
        self.group(en, [fn], reads, writes, strict)

    def group(self, en, fns, reads=(), writes=(), strict=False):
        E = self.eng[en]
        self._apply_waits(en, self._collect(en, reads, writes, strict=strict))
        for fn in fns[:-1]:
            E["stream"].append(lambda e, fn=fn: fn(e))
        E["count"] += 1
        c = E["count"]
        h = E["sem"]
        fn = fns[-1]
        E["stream"].append(lambda e, fn=fn, h=h: fn(e).then_inc(h, 1))
        self._record((E["semname"], c, dict(E["seen"])), reads, writes)

    def dma(self, en, out_ap, in_ap, reads=(), writes=(), sem=None):
        E = self.eng[en]
        h = self.dsem(sem)
        self._apply_waits(en, self._collect(en, reads, writes, strict=True, own_sem=sem))
        self.dcount[sem] += 16
        c = self.dcount[sem]
        E["stream"].append(
            lambda e, h=h, o=out_ap, i=in_ap: e.dma_start(out=o, in_=i).then_inc(h, 16))
        self._record((sem, c, dict(E["seen"])), reads, writes)

    def barrier(self, waiters, on):
        for w in waiters:
            E = self.eng[w]
            need = {}
            for o in on:
                if o == w:
                    continue
                O = self.eng[o]
                if O["count"] > E["seen"].get(O["semname"], 0):
                    need[O["semname"]] = (O["count"], {})
            self._apply_waits(w, need)

    def wait_dma_final(self, en, semnames):
        need = {}
        for sn in semnames:
            if sn in self.dcount and self.dcount[sn] > 0:
                need[sn] = (self.dcount[sn], {})
        self._apply_waits(en, need)


def build(debug=False):
    nc = bass.Bass("TRN2", target_bir_lowering=False)
    xT_d = nc.dram_tensor("xT", [D, S], F32, kind="ExternalInput").ap()
    x_d = nc.dram_tensor("x", [S, D], F32, kind="ExternalInput").ap()
    win_d = nc.dram_tensor("w_in", [D, 10240], F32, kind="ExternalInput").ap()
    wa_d = nc.dram_tensor("w_a", [D, D], F32, kind="ExternalInput").ap()
    wb_d = nc.dram_tensor("w_b", [D, D], F32, kind="ExternalInput").ap()
    wo_d = nc.dram_tensor("w_o", [D, D], F32, kind="ExternalInput").ap()
    cst_d = nc.dram_tensor("cst", [128, NCST], F32, kind="ExternalInput").ap()
    lnp_d = nc.dram_tensor("lnp", [128, 2048], F32, kind="ExternalInput").ap()
    out_d = nc.dram_tensor("out", [S, D], F32, kind="ExternalOutput").ap()
    dbg = {}
    if debug:
        dbg["oT"] = nc.dram_tensor("dbg_oT", [128, 8 * S], BF16, kind="ExternalOutput").ap()
        dbg["cT"] = nc.dram_tensor("dbg_cT", [128, 8 * S], BF16, kind="ExternalOutput").ap()
        dbg["mT"] = nc.dram_tensor("dbg_mT", [128, 8 * S], BF16, kind="ExternalOutput").ap()

    win_v = win_d.rearrange("(dt p) c -> p dt c", p=128)
    wa_v = wa_d.rearrange("(dt p) c -> p dt c", p=128)
    wb_v = wb_d.rearrange("(dt p) c -> p dt c", p=128)
    wo_v = wo_d.rearrange("(dt p) c -> p dt c", p=128)

    cur = [(nc.sbuf_base + 63) // 64 * 64]
    top = nc.sbuf_top

    def sb(name, shape, dt):
        esz = 2 if dt == BF16 else 4
        nb = int(np.prod(shape[1:])) * esz
        a = cur[0]
        t = nc.alloc_sbuf_tensor_at(name, list(shape), dt, offset=a)
        cur[0] = (a + nb + 63) // 64 * 64
        assert cur[0] <= top, (name, cur[0], top)
        return t

    cst = sb("cst", [128, NCST], F32)
    ident = sb("ident", [128, 128], BF16)
    ones_bf = sb("ones_bf", [128, 128], BF16)
    mk = sb("mk", [128, 192], BF16)
    sm = sb("sm", [128, 128], F32)
    xT = sb("xTb", [128, 8, S], BF16)
    oT = sb("oT", [128, 8, S], BF16)
    wt = [sb("wt0", [128, 8, 512], BF16), sb("wt1", [128, 8, 512], BF16)]
    wtx = [sb("wtx0", [128, 8, 512], BF16), sb("wtx1", [128, 8, 512], BF16)]
    R0 = cur[0]
    qm = sb("qm", [128, 4, 2, S], BF16)
    kT = sb("kT", [128, 4, S], BF16)
    vv = sb("vv", [128, NT, 4, 128], BF16)
    gzT = sb("gzT", [128, 4, S], BF16)
    G = [sb("G%d" % i, [128, 512], F32) for i in range(7)]
    rq = [sb("rq%d" % i, [128, 512], BF16) for i in range(3)]
    Eb = [sb("E%d" % i, [128, 512], BF16) for i in range(4)]
    sqb = [sb("sq%d" % i, [128, 256], BF16) for i in range(2)]
    endA = cur[0]
    cur[0] = R0
    cT = sb("cT", [128, 8, S], BF16)
    mT = sb("mT", [128, 8, S], BF16)
    TT = sb("TT", [128, 8, 1032], F32)
    lnp = sb("lnp", [128, 2048], F32)
    endB = cur[0]
    assert max(endA, endB) <= top

    ps = nc.alloc_psum_tensor("ps", [128, 8, 512], F32)
    psb = ps[:, 6:8, :].bitcast(BF16)

    def tts(i):
        return TT[:, i // 2, (i % 2) * 516:(i % 2) * 516 + 516]

    SM_LS, SM_LE, SM_NL, SM_NH, SM_RL1, SM_RL2, SM_SS, SM_RSTD = 0, 2, 4, 8, 16, 20, 24, 28
    SM_BN, SM_MV, SM_RS2, SM_NB, SM_C2, SM_MB, SM_LE2, SM_NM = 32, 68, 74, 80, 84, 85, 86, 88

    stack = ExitStack()
    P = Prog(nc, stack)

    P.dma("sp", cst[:, :], cst_d[:, :], writes=["cst"], sem="d_cst")
    xT_v = xT_d.rearrange("(dt p) t -> p dt t", p=128)

    XR = [(0, 128), (128, 512), (512, 1024), (1024, 1536), (1536, 2048)]

    def xkeys(t0, t1):
        return [("xT", i) for i, (a, b_) in enumerate(XR) if a < t1 and b_ > t0]

    def xT_load(i):
        a, b_ = XR[i]
        P.dma("pool", xT[:, :, a:b_], xT_v[:, :, a:b_], writes=[("xT", i)], sem="d_xT%d" % i)

    P.op("dve", lambda e: e.tensor_copy(out=ident[:, :], in_=cst[:, C_ID:C_ID + 128]),
         reads=["cst"], writes=["ident"])
    P.op("dve", lambda e: e.memset(sm[:, SM_NH:SM_NH + 8], -0.5), writes=["nh"])
    P.op("dve", lambda e: e.memset(ones_bf[:, :], 1.0), writes=["ones"])
    P.op("dve", lambda e: e.memset(mk[0:1, 0:64], 0.0), writes=["mk0"])
    P.op("dve", lambda e: e.memset(mk[0:1, 64:128], 1.0), writes=["mk1"])
    P.op("dve", lambda e: e.memset(mk[0:1, 128:192], -30000.0), writes=["mk2"])
    P.op("dve", lambda e: e.memset(sm[:, SM_C2:SM_C2 + 1], RMS_EPS / (ONE_M_LAM * ONE_M_LAM)), writes=["c2"])
    P.op("dve", lambda e: e.memset(sm[:, SM_LE2:SM_LE2 + 1], LN_EPS), writes=["lneps"])
    P.op("dve", lambda e: e.memset(sm[0:64, SM_MB:SM_MB + 1], 0.0), writes=["mb0"])
    P.op("dve", lambda e: e.memset(sm[64:128, SM_MB:SM_MB + 1], -30000.0), writes=["mb1"])
    lam_a = bass.AP(cst, C_LAM, [[NCST, 128], [128, 2], [1, 64]])
    lam_b = bass.AP(cst, C_LAM + 64, [[NCST, 128], [128, 2], [1, 64]])
    lp = bass.AP(G[0], 0, [[512, 128], [64, 2], [1, 64]])
    P.op("dve", lambda e: e.tensor_tensor(out=lp, in0=lam_a, in1=lam_b, op=ALU.mult),
         reads=["cst"], writes=[("G", 0)])
    P.op("dve", lambda e: e.tensor_reduce(out=sm[:, SM_LS:SM_LS + 2], in_=lp, axis=AX.X, op=ALU.add),
         reads=[("G", 0)], writes=["ls"])
    P.op("act", lambda e: e.activation(out=sm[:, SM_LE:SM_LE + 2], in_=sm[:, SM_LS:SM_LS + 2], func=AF.Exp),
         reads=["ls"], writes=["le"])
    P.op("dve", lambda e: e.tensor_tensor(out=sm[:, SM_NL:SM_NL + 1], in0=sm[:, SM_LE + 1:SM_LE + 2],
                                          in1=sm[:, SM_LE:SM_LE + 1], op=ALU.subtract),
         reads=["le"], writes=["nl0"])
    P.op("dve", lambda e: e.tensor_scalar(out=sm[:, SM_NL + 1:SM_NL + 2], in0=sm[:, SM_NL:SM_NL + 1],
                                          scalar1=-LAM_INIT, scalar2=None, op0=ALU.add),
         reads=["nl0"], writes=["nlam"])
    nlam = sm[:, SM_NL + 1:SM_NL + 2]

    loads = []

    def ld_p1(g, seg, b):
        def f():
            c0 = seg * 1024 + g * 512
            P.dma("pool", wt[b][:, :, :], win_v[:, :, c0:c0 + 512], writes=[("wt", b)], sem="d_wt%d" % b)
        return f

    def ld_p2(cp, b):
        def f():
            for s_ in range(4):
                c0 = 4096 + s_ * 1024 + cp * 256
                dst = (wt[b] if s_ < 2 else wtx[b])[:, :, (s_ % 2) * 256:(s_ % 2) * 256 + 256]
                key, sem = (("wt", b), "d_wt%d" % b) if s_ < 2 else (("wtx", b), "d_wtx%d" % b)
                P.dma("pool", dst, win_v[:, :, c0:c0 + 256], writes=[key], sem=sem)
        return f

    def ld_p4(qd, b):
        def f():
            c0 = qd * 256
            P.dma("pool", wt[b][:, :, 0:256], wa_v[:, :, c0:c0 + 256], writes=[("wt", b)], sem="d_wt%d" % b)
            P.dma("pool", wt[b][:, :, 256:512], wb_v[:, :, c0:c0 + 256], writes=[("wt", b)], sem="d_wt%d" % b)
            P.dma("pool", wtx[b][:, :, 0:256], win_v[:, :, 8192 + c0:8192 + c0 + 256],
                  writes=[("wtx", b)], sem="d_wtx%d" % b)
            P.dma("pool", wtx[b][:, :, 256:512], win_v[:, :, 9216 + c0:9216 + c0 + 256],
                  writes=[("wtx", b)], sem="d_wtx%d" % b)
        return f

    def ld_wo():
        P.dma("pool", wt[0][:, :, :], wo_v[:, :, 0:512], writes=[("wt", 0)], sem="d_wt0")
        P.dma("pool", wtx[0][:, :, :], wo_v[:, :, 512:1024], writes=[("wtx", 0)], sem="d_wtx0")

    for g in range(2):
        for seg in range(4):
            loads.append(ld_p1(g, seg, seg % 2))
    for cp in range(4):
        loads.append(ld_p2(cp, cp % 2))
    for qd in range(4):
        loads.append(ld_p4(qd, qd % 2))
    loads.append(ld_wo)
    nload = [0]

    pf_limit = [99]

    def prefetch(upto):
        upto = min(upto, pf_limit[0])
        while nload[0] <= upto and nload[0] < len(loads):
            loads[nload[0]]()
            nload[0] += 1

    prefetch(0)
    xT_load(0)
    xT_load(1)
    prefetch(1)
    for i_ in range(2, 5):
        xT_load(i_)
    P.op("pool", lambda e: e.memset(qm[64:128, :, 0, :], 0.0), writes=["qz0"])
    P.op("pool", lambda e: e.memset(qm[0:64, :, 1, :], 0.0), writes=["qz1"])
    ucnt = [0]

    pend1 = []

    def flush1(keep=0):
        while len(pend1) > keep:
            pend1.pop(0)()

    def p1_unit(g, seg, b, t):
        u = ucnt[0]
        ucnt[0] += 1
        bank = u % 2
        p = u % 2
        p3 = u % 3
        pst = ps[:, bank, :]
        fns = []
        for dt_ in range(8):
            fns.append(lambda e, dt_=dt_: e.matmul(
                pst, xT[:, dt_, t * 128:(t + 1) * 128], wt[b][:, dt_, :],
                start=(dt_ == 0), stop=(dt_ == 7)))
        P.group("pe", fns, reads=xkeys(t * 128, t * 128 + 128) + [("wt", b)], writes=[("ps", bank)])
        flush1(keep=1)
        if seg < 2:
            p4 = bass.AP(ps, bank * 512, [[4096, 128], [64, 8], [32, 2], [1, 32]])
            cosb = bass.AP(cst, C_COS + t * 32, [[NCST, 128], [0, 8], [0, 2], [1, 32]])
            sinb = bass.AP(cst, C_SIN + t * 32, [[NCST, 128], [0, 8], [1, 32]])
            a4 = bass.AP(G[p], 0, [[512, 128], [64, 8], [32, 2], [1, 32]])
            P.op("dve", lambda e: e.tensor_tensor(out=a4, in0=p4, in1=cosb, op=ALU.mult),
                 reads=[("ps", bank), "cst"], writes=[("G", p)])
            pt2 = bass.AP(ps, bank * 512 + 32, [[4096, 128], [64, 8], [1, 32]])
            pt1 = bass.AP(ps, bank * 512, [[4096, 128], [64, 8], [1, 32]])
            b1 = bass.AP(G[2 + p], 0, [[512, 128], [32, 8], [1, 32]])
            b2 = bass.AP(G[2 + p], 256, [[512, 128], [32, 8], [1, 32]])
            P.op("dve", lambda e: e.tensor_tensor(out=b1, in0=pt2, in1=sinb, op=ALU.mult),
                 reads=[("ps", bank), "cst"], writes=[("Ta", p)])
            P.op("dve", lambda e: e.tensor_tensor(out=b2, in0=pt1, in1=sinb, op=ALU.mult),
                 reads=[("ps", bank), "cst"], writes=[("Tb", p)])
            a1 = bass.AP(G[p], 0, [[512, 128], [64, 8], [1, 32]])
            a2 = bass.AP(G[p], 32, [[512, 128], [64, 8], [1, 32]])
            r1 = bass.AP(rq[p3], 0, [[512, 128], [64, 8], [1, 32]])
            r2 = bass.AP(rq[p3], 32, [[512, 128], [64, 8], [1, 32]])
            P.op("pool", lambda e: e.tensor_tensor(out=r1, in0=a1, in1=b1, op=ALU.subtract),
                 reads=[("G", p), ("Ta", p)], writes=[("rq", p3)])
            P.op("pool", lambda e: e.tensor_tensor(out=r2, in0=a2, in1=b2, op=ALU.add),
                 reads=[("G", p), ("Tb", p)], writes=[("rq", p3)])
            tb = u % 2

            def later():
                fns2 = []
                for hh in range(4):
                    fns2.append(lambda e, hh=hh: e.transpose(
                        psb[:, tb, hh * 128:(hh + 1) * 128], rq[p3][:, hh * 128:(hh + 1) * 128], ident[:, :]))
                P.group("pe", fns2, reads=[("rq", p3), "ident"], writes=[("ps", 6 + tb)])
                if seg == 0:
                    for c in range(2):
                        srcv = psb[c * 64:(c + 1) * 64, tb, 0:512].rearrange("p (h q) -> p h q", h=4)
                        dstv = qm[c * 64:(c + 1) * 64, :, c, t * 128:(t + 1) * 128]
                        P.op("act", lambda e, srcv=srcv, dstv=dstv: e.activation(out=dstv, in_=srcv, func=AF.Copy),
                             reads=[("ps", 6 + tb), "qz0", "qz1"], writes=[("qT", t, c)])
                else:
                    srcv = psb[:, tb, 0:512].rearrange("p (h q) -> p h q", h=4)
                    P.op("act", lambda e: e.activation(
                        out=kT[:, :, t * 128:(t + 1) * 128], in_=srcv, func=AF.Copy),
                        reads=[("ps", 6 + tb)], writes=[("kT", t)])
            pend1.append(later)
        else:
            P.op("act", lambda e: e.activation(out=vv[:, t, :, :].rearrange("p h e -> p (h e)"), in_=pst, func=AF.Copy),
                 reads=[("ps", bank)], writes=[("v", t)])

    def p1z_unit(g, b, hh, tc):
        u = ucnt[0]
        ucnt[0] += 1
        bank = u % 2
        p = u % 2
        pst = ps[:, bank, :]
        fns = []
        for dt_ in range(8):
            fns.append(lambda e, dt_=dt_: e.matmul(
                pst, wt[b][:, dt_, hh * 128:(hh + 1) * 128], xT[:, dt_, tc * 512:(tc + 1) * 512],
                start=(dt_ == 0), stop=(dt_ == 7)))
        P.group("pe", fns, reads=xkeys(tc * 512, tc * 512 + 512) + [("wt", b)], writes=[("ps", bank)])
        flush1(keep=1)
        P.op("act", lambda e: e.activation(out=G[4 + p][:, :], in_=pst, func=AF.Silu),
             reads=[("ps", bank)], writes=[("G4a", p), ("G4b", p)])
        P.op("pool", lambda e: e.tensor_scalar(
            out=gzT[:, hh, tc * 512:(tc + 1) * 512], in0=G[4 + p][:, :], scalar1=cst[:, C_GC:C_GC + 1], scalar2=1.0,
            op0=ALU.mult, op1=ALU.mult),
            reads=[("G4a", p), ("G4b", p), "cst"], writes=[("gz", hh, tc)])

    def phase1(g, li0):
        for seg in range(4):
            prefetch(li0 + seg + 1)
            if seg < 3:
                for t in range(NT):
                    p1_unit(g, seg, seg % 2, t)
            else:
                for hh in range(4):
                    for tc in range(4):
                        p1z_unit(g, seg % 2, hh, tc)
        flush1()

    def phase3(g):
        steps = []
        for hh in range(4):
            for hc in (0, 7, 1, 6, 2, 5, 3, 4):
                for j in range(2 * hc + 2):
                    steps.append((hh, hc, j))
        n = len(steps)
        pend = []
        unit_of = {}
        sctr = [0]

        def next_sbank():
            bk = (0, 1, 6, 7)[sctr[0] % 4]
            sctr[0] += 1
            return bk

        def emit_S(i):
            hh, hc, j = steps[i]
            sbk = next_sbank()
            eb = i % 4
            r = max(0, j - 2 * hc)
            N = 256 - 128 * r
            q0 = 256 * hc + 128 * r
            so = bass.AP(ps, sbk * 512 + 128 * r, [[4096, 128], [256, 2], [1, N]])
            diag = j >= 2 * hc
            if r == 0:
                sfns = [lambda e: e.matmul(
                    so, kT[:, hh, j * 128:(j + 1) * 128], qm[:, hh, :, q0:q0 + N], start=True, stop=True)]
            else:
                sfns = [lambda e, c=c: e.matmul(
                    ps[:, sbk, c * 256 + 128:c * 256 + 256], kT[:, hh, j * 128:(j + 1) * 128], qm[:, hh, c, q0:q0 + N],
                    start=(c == 0), stop=(c == 1), skip_group_check=True) for c in range(2)]
            P.group("pe", sfns,
                reads=[("kT", j), "qz0", "qz1"] + [("qT", 2 * hc + qt, c) for qt in range(r, 2) for c in range(2)],
                writes=[("ps", sbk)])
            if diag:
                sa = bass.AP(ps, sbk * 512 + 128 * r, [[4096, 128], [256, 2], [1, 64]])
                ea = bass.AP(Eb[eb], 128 * r, [[512, 128], [256, 2], [1, 64]])
                sb2 = bass.AP(ps, sbk * 512 + 128 * r + 64, [[4096, 128], [256, 2], [1, N - 64]])
                eb2 = bass.AP(Eb[eb], 128 * r + 64, [[512, 128], [256, 2], [1, N - 64]])
                P.op("act", lambda e: e.activation(out=ea, in_=sa, func=AF.Exp, scale=0.125, bias=sm[:, SM_MB:SM_MB + 1]),
                     reads=[("ps", sbk), "mb0", "mb1"], writes=[("E", eb)])
                P.op("act", lambda e: e.activation(out=eb2, in_=sb2, func=AF.Exp, scale=0.125),
                     reads=[("ps", sbk)], writes=[("E", eb)])
            else:
                ev = bass.AP(Eb[eb], 128 * r, [[512, 128], [256, 2], [1, N]])
                P.op("act", lambda e: e.activation(out=ev, in_=so, func=AF.Exp, scale=0.125),
                     reads=[("ps", sbk)], writes=[("E", eb)])

        def emit_AV(i):
            hh, hc, j = steps[i]
            if (hh, hc) not in unit_of:
                unit_of[(hh, hc)] = ucnt[0]
                ucnt[0] += 1
            u = unit_of[(hh, hc)]
            p = u % 2
            eb = i % 4
            r = max(0, j - 2 * hc)
            N = 256 - 128 * r
            ev = bass.AP(Eb[eb], 128 * r, [[512, 128], [256, 2], [1, N]])
            oo = bass.AP(ps, (2 + 2 * p) * 512 + 128 * r, [[4096, 128], [256, 2], [1, N]])
            lo = bass.AP(ps, (3 + 2 * p) * 512 + 128 * r, [[4096, 128], [256, 2], [1, N]])
            st, sp_ = (j == 0), (j == 2 * hc + 1)
            if r == 0:
                afns = [
                    lambda e: e.matmul(oo, vv[:, j, hh, :], ev, start=st, stop=sp_, skip_group_check=True),
                    lambda e: e.matmul(lo, ones_bf[:, :], ev, start=st, stop=sp_, skip_group_check=True)]
            else:
                afns = []
                for c in range(2):
                    cs = slice(c * 256 + 128, c * 256 + 256)
                    afns.append(lambda e, cs=cs: e.matmul(ps[:, 2 + 2 * p, cs], vv[:, j, hh, :], Eb[eb][:, cs],
                                                          start=False, stop=sp_, skip_group_check=True))
                    afns.append(lambda e, cs=cs: e.matmul(ps[:, 3 + 2 * p, cs], ones_bf[:, :], Eb[eb][:, cs],
                                                          start=False, stop=sp_, skip_group_check=True))
            P.group("pe", afns,
                reads=[("E", eb), ("v", j), "ones"], writes=[("ps", 2 + 2 * p), ("ps", 3 + 2 * p)])
            if sp_:
                post(i, hh, hc, u)

        def post(i, hh, hc, u):
            p = u % 2
            ru = u % 2
            hg = 4 * g + hh
            ob_, lb_ = 2 + 2 * p, 3 + 2 * p
            Ls = G[ru]
            T = G[2 + ru]
            o_ = G[4 + ru][:, 0:256]
            on_ = G[4 + ru][:, 256:512]
            rs_ = G[6][:, ru * 256:(ru + 1) * 256]
            ko, kn, kr = ("G4a", ru), ("G4b", ru), ("G", 6, ru)
            P.op("dve", lambda e: e.tensor_copy(out=Ls[:, :], in_=ps[:, lb_, :]),
                 reads=[("ps", lb_)], writes=[("G", ru)], strict=True)
            P.op("dve", lambda e: e.tensor_tensor(out=T[:, 0:256], in0=ps[:, ob_, 0:256], in1=Ls[:, 256:512], op=ALU.mult),
                 reads=[("ps", ob_), ("G", ru)], writes=[("Ta", ru)], strict=True)
            P.op("dve", lambda e: e.tensor_tensor(out=T[:, 256:512], in0=ps[:, ob_, 256:512], in1=Ls[:, 0:256], op=ALU.mult),
                 reads=[("ps", ob_), ("G", ru)], writes=[("Tb", ru)], strict=True)
            P.op("dve", lambda e: e.scalar_tensor_tensor(out=o_, in0=T[:, 256:512], scalar=nlam, in1=T[:, 0:256],
                                                         op0=ALU.mult, op1=ALU.add),
                 reads=[("Ta", ru), ("Tb", ru), "nlam"], writes=[ko], strict=True)
            P.op("pool", lambda e: e.tensor_tensor(out=sqb[ru][:, :], in0=o_, in1=o_, op=ALU.mult),
                 reads=[ko], writes=[("sq", ru)], strict=True)
            c1 = 1.0 / (128.0 * ONE_M_LAM * ONE_M_LAM)
            c2 = RMS_EPS / (ONE_M_LAM * ONE_M_LAM)
            P.op("dve", lambda e: e.tensor_tensor(out=on_, in0=Ls[:, 0:256], in1=Ls[:, 256:512], op=ALU.mult),
                 reads=[("G", ru)], writes=[kn], strict=True)
            P.op("dve", lambda e: e.scalar_tensor_tensor(out=rs_, in0=on_, scalar=c2, in1=on_,
                                                         op0=ALU.mult, op1=ALU.mult),
                 reads=[kn], writes=[kr], strict=True)
            q0 = 256 * hc

            rb = {}

            def later1():
                rbk = next_sbank()
                rb["bk"] = rbk
                ssp = ps[:, rbk, 0:256]
                P.group("pe", [lambda e: e.matmul(ssp, ones_bf[:, :], sqb[ru][:, :], start=True, stop=True, skip_group_check=True)],
                        reads=[("sq", ru), "ones"], writes=[("ps", rbk)])

            def later2():
                rbk = rb["bk"]
                ssp = ps[:, rbk, 0:256]
                P.op("dve", lambda e: e.scalar_tensor_tensor(out=rs_, in0=ssp, scalar=c1, in1=rs_,
                                                             op0=ALU.mult, op1=ALU.add),
                     reads=[("ps", rbk), kr], writes=[kr], strict=True)
                P.op("act", lambda e: e.activation(out=rs_, in_=rs_, func=AF.Ln),
                     reads=[kr], writes=[kr], strict=True)
                P.op("act", lambda e: e.activation(out=rs_, in_=rs_, func=AF.Exp, scale=-0.5),
                     reads=[kr], writes=[kr], strict=True)
                P.op("pool", lambda e: e.tensor_tensor(out=on_, in0=o_, in1=rs_, op=ALU.mult),
                     reads=[ko, kr], writes=[kn], strict=True)
                P.op("pool", lambda e: e.tensor_tensor(out=oT[:, hg, q0:q0 + 256], in0=on_, in1=gzT[:, hh, q0:q0 + 256], op=ALU.mult),
                     reads=[kn] + [("gz", hh, tc) for tc in range(4)], writes=[("oT", hg)], strict=True)
            pend.append((i + 9, later1))
            pend.append((i + 11, later2))

        def flush(upto):
            k = 0
            while k < len(pend):
                if pend[k][0] <= upto:
                    pend.pop(k)[1]()
                else:
                    k += 1

        emit_S(0)
        emit_S(1)
        emit_S(2)
        for i in range(n):
            if i + 3 < n:
                emit_S(i + 3)
            emit_AV(i)
            flush(i)
        flush(10 ** 9)

    def p2_unit(b, ct, sub, tc):
        u = ucnt[0]
        ucnt[0] += 1
        pset = (u % 2) * 4
        tset = (u % 2) * 5
        for s_ in range(4):
            wsrc = wt[b] if s_ < 2 else wtx[b]
            wkey = ("wt", b) if s_ < 2 else ("wtx", b)
            c0 = (s_ % 2) * 256 + sub * 128
            fns = []
            for dt_ in range(8):
                fns.append(lambda e, dt_=dt_, wsrc=wsrc, c0=c0, s_=s_: e.matmul(
                    ps[:, pset + s_, :], wsrc[:, dt_, c0:c0 + 128], xT[:, dt_, tc * 512:(tc + 1) * 512],
                    start=(dt_ == 0), stop=(dt_ == 7)))
            P.group("pe", fns, reads=xkeys(tc * 512, tc * 512 + 512) + [wkey], writes=[("ps", pset + s_)])
        hs, ut, acc, sz, tm = [tts(tset + k) for k in range(5)]
        kh, ku, ka, ks, kt = [("TT", tset + k) for k in range(5)]
        P.op("act", lambda e: e.activation(out=hs[:, 0:512], in_=ps[:, pset + 0, :], func=AF.Copy),
             reads=[("ps", pset)], writes=[kh])
        if tc == 0:
            P.op("pool", lambda e: e.memset(ut[:, 0:2], 0.0), writes=[ku])
        else:
            pi = ((u - 1) % 2) * 5 + 1
            prev = tts(pi)
            P.op("pool", lambda e: e.tensor_copy(out=ut[:, 0:2], in_=prev[:, 512:514]),
                 reads=[("TT", pi)], writes=[ku])
        P.op("dve", lambda e: e.tensor_tensor(out=ut[:, 2:514], in0=ps[:, pset + 2, :], in1=hs[:, 0:512], op=ALU.mult),
             reads=[("ps", pset + 2), kh, ku], writes=[ku])
        cw = [cst[:, C_CW + ct * 3 + k:C_CW + ct * 3 + k + 1] for k in range(3)]
        cb = cst[:, C_CB + ct:C_CB + ct + 1]
        P.op("dve", lambda e: e.tensor_scalar(out=acc[:, 0:512], in0=ut[:, 2:514], scalar1=cw[2], scalar2=cb,
                                              op0=ALU.mult, op1=ALU.add),
             reads=[ku, "cst"], writes=[ka])
        P.op("dve", lambda e: e.scalar_tensor_tensor(out=acc[:, 0:512], in0=ut[:, 1:513], scalar=cw[1], in1=acc[:, 0:512],
                                                     op0=ALU.mult, op1=ALU.add),
             reads=[ku, ka, "cst"], writes=[ka])
        P.op("dve", lambda e: e.scalar_tensor_tensor(out=acc[:, 0:512], in0=ut[:, 0:512], scalar=cw[0], in1=acc[:, 0:512],
                                                     op0=ALU.mult, op1=ALU.add),
             reads=[ku, ka, "cst"], writes=[ka])
        P.op("act", lambda e: e.activation(out=sz[:, 0:512], in_=ps[:, pset + 3, :], func=AF.Silu),
             reads=[("ps", pset + 3)], writes=[ks])
        P.op("dve", lambda e: e.tensor_tensor(out=tm[:, 0:512], in0=ps[:, pset + 1, :], in1=acc[:, 0:512], op=ALU.mult),
             reads=[("ps", pset + 1), ka], writes=[kt])
        P.op("pool", lambda e: e.tensor_tensor(out=cT[:, ct, tc * 512:(tc + 1) * 512], in0=tm[:, 0:512], in1=sz[:, 0:512], op=ALU.mult),
             reads=[kt, ks], writes=[("cT", ct)])

    def phase2(li0):
        for cp in range(4):
            prefetch(li0 + cp + 1)
            for sub in range(2):
                for tc in range(4):
                    p2_unit(cp % 2, 2 * cp + sub, sub, tc)

    def p4a_unit(b, dt_o, sub, tc):
        u = ucnt[0]
        ucnt[0] += 1
        pset = (u % 2) * 4
        tset = (u % 2) * 5
        specs = [(wt[b], ("wt", b), 0, oT, [("oT", k) for k in range(8)]),
                 (wt[b], ("wt", b), 256, cT, [("cT", k) for k in range(8)]),
                 (wtx[b], ("wtx", b), 0, xT, xkeys(tc * 512, tc * 512 + 512)),
                 (wtx[b], ("wtx", b), 256, xT, xkeys(tc * 512, tc * 512 + 512))]
        for s_, (wsrc, wkey, cb0, src_, skeys) in enumerate(specs):
            c0 = cb0 + sub * 128
            fns = []
            for k in range(8):
                fns.append(lambda e, k=k, wsrc=wsrc, c0=c0, s_=s_, src_=src_: e.matmul(
                    ps[:, pset + s_, :], wsrc[:, k, c0:c0 + 128], src_[:, k, tc * 512:(tc + 1) * 512],
                    start=(k == 0), stop=(k == 7)))
            P.group("pe", fns, reads=skeys + [wkey], writes=[("ps", pset + s_)])
        sa, sb_, m1, m2 = [tts(tset + k) for k in range(4)]
        ksa, ksb, km1, km2 = [("TT", tset + k) for k in range(4)]
        bga = cst[:, C_BG + dt_o:C_BG + dt_o + 1]
        bgb = cst[:, C_BG + 8 + dt_o:C_BG + 8 + dt_o + 1]
        P.op("act", lambda e: e.activation(out=sa[:, 0:512], in_=ps[:, pset + 2, :], func=AF.Sigmoid, bias=bga),
             reads=[("ps", pset + 2), "cst"], writes=[ksa])
        P.op("act", lambda e: e.activation(out=sb_[:, 0:512], in_=ps[:, pset + 3, :], func=AF.Sigmoid, bias=bgb),
             reads=[("ps", pset + 3), "cst"], writes=[ksb])
        P.op("dve", lambda e: e.tensor_tensor(out=m1[:, 0:512], in0=ps[:, pset + 0, :], in1=sa[:, 0:512], op=ALU.mult),
             reads=[("ps", pset + 0), ksa], writes=[km1])
        P.op("dve", lambda e: e.tensor_tensor(out=m2[:, 0:512], in0=ps[:, pset + 1, :], in1=sb_[:, 0:512], op=ALU.mult),
             reads=[("ps", pset + 1), ksb], writes=[km2])
        P.op("pool", lambda e: e.tensor_tensor(out=mT[:, dt_o, tc * 512:(tc + 1) * 512], in0=m1[:, 0:512], in1=m2[:, 0:512], op=ALU.add),
             reads=[km1, km2], writes=[("mT", dt_o)])

    def phase4a(li0):
        for qd in range(4):
            prefetch(li0 + qd + 1)
            for sub in range(2):
                for tc in range(4):
                    p4a_unit(qd % 2, 2 * qd + sub, sub, tc)

    mkeys = [("mT", k) for k in range(8)]

    def xload(t):
        k = t % 3
        P.dma("sp", TT[:, k, 0:1024], x_d[t * 128:(t + 1) * 128, :],
              writes=[("TT", 2 * k), ("TT", 2 * k + 1)], sem="d_x%d" % k)

    def p4b_A(t):
        k = 3 + (t % 5)
        kx = t % 3
        xk = [("TT", 2 * kx), ("TT", 2 * kx + 1)]
        pb = (t % 4) * 2
        for n_ in range(2):
            wsrc = wt[0] if n_ == 0 else wtx[0]
            wkey = ("wt", 0) if n_ == 0 else ("wtx", 0)
            fns = []
            for kk in range(8):
                fns.append(lambda e, kk=kk, wsrc=wsrc, n_=n_: e.matmul(
                    ps[:, pb + n_, :], mT[:, kk, t * 128:(t + 1) * 128], wsrc[:, kk, :],
                    start=(kk == 0), stop=(kk == 7)))
            P.group("pe", fns, reads=mkeys + [wkey], writes=[("ps", pb + n_)])
        yk = [("TT", 2 * k), ("TT", 2 * k + 1)]
        y = TT[:, k, 0:1024]
        psv = bass.AP(ps, pb * 512, [[4096, 128], [1, 1024]])
        P.op("dve", lambda e: e.scalar_tensor_tensor(out=y, in0=TT[:, kx, 0:1024], scalar=DN_ALPHA, in1=psv,
                                                     op0=ALU.mult, op1=ALU.add),
             reads=xk + [("ps", pb), ("ps", pb + 1)], writes=yk)
        s2 = t % 3
        so = SM_BN + s2 * 12
        kb = ("bn", s2)
        P.op("dve", lambda e: e.bn_stats(out=sm[:, so:so + 6], in_=TT[:, k, 0:512]), reads=yk, writes=[kb])
        P.op("dve", lambda e: e.bn_stats(out=sm[:, so + 6:so + 12], in_=TT[:, k, 512:1024]), reads=yk, writes=[kb])
        mo = SM_MV + s2 * 2
        P.op("dve", lambda e: e.bn_aggr(out=sm[:, mo:mo + 2], in_=sm[:, so:so + 12]),
             reads=[kb], writes=[("mv", s2)])
        ro = SM_RS2 + s2 * 2
        P.op("dve", lambda e: e.tensor_scalar(out=sm[:, SM_NM + s2:SM_NM + s2 + 1], in0=sm[:, mo:mo + 1], scalar1=-1.0, scalar2=None, op0=ALU.mult),
             reads=[("mv", s2)], writes=[("nm", s2)])

    def p4b_A2(t):
        s2 = t % 3
        mo = SM_MV + s2 * 2
        ro = SM_RS2 + s2 * 2
        P.op("act", lambda e: e.activation(out=sm[:, ro:ro + 1], in_=sm[:, mo + 1:mo + 2], func=AF.Ln, bias=sm[:, SM_LE2:SM_LE2 + 1]),
             reads=[("mv", s2), "lneps"], writes=[("ve", s2)])
        P.op("act", lambda e: e.activation(out=sm[:, ro + 1:ro + 2], in_=sm[:, ro:ro + 1], func=AF.Exp, scale=-0.5),
             reads=[("ve", s2)], writes=[("rs", s2)])

    def p4b_B(t):
        k = 3 + (t % 5)
        s2 = t % 3
        yk = [("TT", 2 * k), ("TT", 2 * k + 1)]
        y = TT[:, k, 0:1024]
        mo = SM_MV + s2 * 2
        ro = SM_RS2 + s2 * 2
        no = SM_NB + s2
        P.op("act", lambda e: e.activation(out=sm[:, no:no + 1], in_=sm[:, SM_NM + s2:SM_NM + s2 + 1], func=AF.Copy, scale=sm[:, ro + 1:ro + 2]),
             reads=[("nm", s2), ("rs", s2)], writes=[("nb", s2)])
        P.op("act", lambda e: e.activation(out=y, in_=y, func=AF.Identity, scale=sm[:, ro + 1:ro + 2], bias=sm[:, no:no + 1]),
             reads=yk + [("rs", s2), ("nb", s2)], writes=yk)

    def p4b_C(t):
        k = 3 + (t % 5)
        yk = [("TT", 2 * k), ("TT", 2 * k + 1)]
        y = TT[:, k, 0:1024]
        P.op("dve", lambda e: e.tensor_tensor(out=y, in0=y, in1=lnp[:, 0:1024], op=ALU.mult),
             reads=yk + ["lnp"], writes=yk)
        P.op("pool", lambda e: e.tensor_tensor(out=y, in0=y, in1=lnp[:, 1024:2048], op=ALU.add),
             reads=yk + ["lnp"], writes=yk)
        P.dma("sp", out_d[t * 128:(t + 1) * 128, :], y, reads=yk, sem="d_o%d" % (k - 3))

    def phase4b():
        for t in range(2):
            xload(t)
        for t in range(NT + 2):
            if t + 2 < NT:
                xload(t + 2)
            if t < NT:
                p4b_A(t)
            if 1 <= t <= NT:
                p4b_B(t - 1)
            if t < NT:
                p4b_A2(t)
            if 2 <= t <= NT + 1:
                p4b_C(t - 2)

    phase1(0, 0)
    phase3(0)
    phase1(1, 4)
    phase3(1)
    if debug:
        P.dma("sp", dbg["oT"], oT[:, :, :].rearrange("p a b -> p (a b)"), reads=[("oT", k) for k in range(8)], sem="d_dbg0")
    P.barrier(["act", "dve", "pool", "sp"], ["pe", "act", "dve", "pool"])
    P.dma("sp", lnp[:, :], lnp_d[:, :], writes=["lnp"], sem="d_lnp")
    phase2(8)
    phase4a(12)
    if debug:
        P.dma("sp", dbg["cT"], cT[:, :, :].rearrange("p a b -> p (a b)"), reads=[("cT", k) for k in range(8)], sem="d_dbg1")
        P.dma("sp", dbg["mT"], mT[:, :, :].rearrange("p a b -> p (a b)"), reads=[("mT", k) for k in range(8)], sem="d_dbg2")
    phase4b()
    P.wait_dma_final("sp", ["d_o0", "d_o1", "d_o2", "d_o3", "d_o4", "d_dbg0", "d_dbg1", "d_dbg2"])

    with nc.allow_low_precision("bf16 matmul operands, fp32 accumulation"):
        with nc.Block() as block:
            @block.tensor
            def _(e):
                for f in P.eng["pe"]["stream"]:
                    f(e)

            @block.scalar
            def _(e):
                for f in P.eng["act"]["stream"]:
                    f(e)

            @block.vector
            def _(e):
                for f in P.eng["dve"]["stream"]:
                    f(e)

            @block.gpsimd
            def _(e):
                for f in P.eng["pool"]["stream"]:
                    f(e)

            @block.sync
            def _(e):
                for f in P.eng["sp"]["stream"]:
                    f(e)
    stack.close()
    return nc


def make_consts(inputs):
    cst = np.zeros((128, NCST), np.float32)
    half = 32
    inv_freq = (1.0 / (np.float32(10000.0) ** (np.arange(half, dtype=np.float32) / np.float32(half)))).astype(np.float32)
    pos = np.arange(S, dtype=np.float32)
    ang = (pos[:, None] * inv_freq[None, :]).astype(np.float32)
    cos = np.cos(ang).astype(np.float32).reshape(NT, 128, half).transpose(1, 0, 2).reshape(128, NT * half)
    sin = np.sin(ang).astype(np.float32).reshape(NT, 128, half).transpose(1, 0, 2).reshape(128, NT * half)
    cst[:, C_COS:C_COS + 512] = cos
    cst[:, C_SIN:C_SIN + 512] = sin
    cst[:, C_GC] = inputs["subln_g"][0]
    cst[:, C_BG:C_BG + 16] = inputs["b_gate"][0].reshape(16, 128).T
    cst[:, C_CW:C_CW + 24] = inputs["conv_w"][0].reshape(3, 8, 128).transpose(2, 1, 0).reshape(128, 24)
    cst[:, C_CB:C_CB + 8] = inputs["conv_b"][0].reshape(8, 128).T
    lam = np.concatenate([inputs["lambda_q1"][0], inputs["lambda_k1"][0],
                          inputs["lambda_q2"][0], inputs["lambda_k2"][0]])
    cst[:, C_LAM:C_LAM + 256] = lam[None, :]
    cst[:, C_ID:C_ID + 128] = np.eye(128, dtype=np.float32)
    return cst


_NC_CACHE = {}


def kernel(**inputs):
    inputs = {k: np.asarray(v) for k, v in inputs.items()}
    x = inputs["x"].astype(np.float32, copy=False)
    B = x.shape[0]
    if "nc" not in _NC_CACHE:
        _NC_CACHE["nc"] = build()
    nc = _NC_CACHE["nc"]
    cst = make_consts(inputs)
    lnp = np.ascontiguousarray(np.broadcast_to(
        np.concatenate([inputs["ln_g"][0], inputs["ln_b"][0]]).astype(np.float32)[None, :], (128, 2048)))
    w_in = np.ascontiguousarray(inputs["w_in"][0], dtype=np.float32)
    w_a = np.ascontiguousarray(inputs["w_a_out"][0], dtype=np.float32)
    w_b = np.ascontiguousarray(inputs["w_b_out"][0], dtype=np.float32)
    w_o = np.ascontiguousarray(inputs["w_o"][0], dtype=np.float32)
    in_maps = []
    for b in range(B):
        in_maps.append({
            "xT": np.ascontiguousarray(x[b].T),
            "x": np.ascontiguousarray(x[b]),
            "w_in": w_in, "w_a": w_a, "w_b": w_b, "w_o": w_o, "cst": cst, "lnp": lnp,
        })
    res = run_bass_kernel_spmd(nc, in_maps, core_ids=list(range(B)))
    return np.stack([np.asarray(r["out"]) for r in res.results], axis=0).astype(np.float32)
```
